# Optimizing a Trainium2 kernel written in Bass

```python
import math
import jax, jax.numpy as jnp
from jax import lax
import numpy as np

D_MODEL = 1024
BATCH = 8
SEQ = 4096
DEPTH = 4
DEC_BATCH = 4
DEC_SEQ = 8192
PAST_LEN = 128

GRID_W = 64
HEAD_DIM = 64
N_HEADS = 8
N_KV = 2
GROUP = N_HEADS // N_KV
ATTN_W = N_HEADS * HEAD_DIM
KV_W = N_KV * HEAD_DIM
CONV_W = D_MODEL // 2
MIX_W = ATTN_W + CONV_W
IN_W = ATTN_W + 2 * KV_W + 3 * CONV_W
SPLITS = (ATTN_W, ATTN_W + KV_W, ATTN_W + 2 * KV_W,
          ATTN_W + 2 * KV_W + CONV_W, ATTN_W + 2 * KV_W + 2 * CONV_W)
D_FF = 2816
CONV_K = 3
Q_BLOCK = 128
ROPE_THETA = 10000.0
AXIS_PAIRS = HEAD_DIM // 4
EPS = 1e-6
ALPHA = (2.0 * DEPTH) ** 0.25
BETA = (8.0 * DEPTH) ** -0.25

kernel_name = "hybrid_parallel_conv_gqa_axial_encoder"


def _layernorm(x, g, b):
    xf = x.astype(jnp.float32)
    mu = jnp.mean(xf, axis=-1, keepdims=True)
    xc = xf - mu
    var = jnp.mean(xc * xc, axis=-1, keepdims=True)
    return (xc * lax.rsqrt(var + EPS) * g.astype(jnp.float32) + b.astype(jnp.float32)).astype(x.dtype)


def _rmsnorm_f32(x, g):
    xf = x.astype(jnp.float32)
    return xf * lax.rsqrt(jnp.mean(xf * xf, axis=-1, keepdims=True) + EPS) * g.astype(jnp.float32)


def _conv3(x, w, b):
    xp = jnp.pad(x, ((0, 0), (1, 1), (0, 0)))
    return xp[:, :-2] * w[0] + xp[:, 1:-1] * w[1] + xp[:, 2:] * w[2] + b


def _axial_rope_tables(seq_len):
    rows = seq_len // GRID_W
    row = jnp.repeat(jnp.arange(rows, dtype=jnp.float32), GRID_W)
    col = jnp.tile(jnp.arange(GRID_W, dtype=jnp.float32), rows)
    inv = ROPE_THETA ** (-jnp.arange(AXIS_PAIRS, dtype=jnp.float32) / AXIS_PAIRS)
    ang = jnp.concatenate([row[:, None] * inv, col[:, None] * inv], axis=-1)
    return jnp.cos(ang), jnp.sin(ang)


def _apply_rope(x, cos, sin):
    xr = x.reshape(x.shape[:-1] + (HEAD_DIM // 2, 2))
    x0, x1 = xr[..., 0], xr[..., 1]
    c = cos[None, :, None, :]
    s = sin[None, :, None, :]
    return jnp.stack([x0 * c - x1 * s, x0 * s + x1 * c], axis=-1).reshape(x.shape)


def _gqa_blocked(q, k, v):
    bsz, seq = q.shape[0], q.shape[1]
    nblk = seq // Q_BLOCK
    qb = q.reshape(bsz, nblk, Q_BLOCK, N_KV, GROUP, HEAD_DIM).transpose(1, 0, 2, 3, 4, 5)
    scale = HEAD_DIM ** -0.5

    def one_block(qi):
        s = jnp.einsum('bqkgd,bskd->bkgqs', qi, k, preferred_element_type=jnp.float32) * scale
        p = jax.nn.softmax(s, axis=-1).astype(v.dtype)
        return jnp.einsum('bkgqs,bskd->bqkgd', p, v)

    o = lax.map(one_block, qb)
    return o.transpose(1, 0, 2, 3, 4, 5).reshape(bsz, seq, ATTN_W)


def _layer(x, cos, sin, w_in, q_norm, k_norm, conv_w, conv_b, w_o, ln1_g, ln1_b,
           w_up, ffn_conv_w, ffn_conv_b, w_down, ln2_g, ln2_b):
    bsz, seq, _ = x.shape
    z = x @ w_in
    q, k, v, c_b, c_c, c_h = jnp.split(z, SPLITS, axis=-1)
    q = _apply_rope(_rmsnorm_f32(q.reshape(bsz, seq, N_HEADS, HEAD_DIM), q_norm), cos, sin)
    q = q.astype(x.dtype).reshape(bsz, seq, N_KV, GROUP, HEAD_DIM)
    k = _apply_rope(_rmsnorm_f32(k.reshape(bsz, seq, N_KV, HEAD_DIM), k_norm), cos, sin).astype(x.dtype)
    v = v.reshape(bsz, seq, N_KV, HEAD_DIM)
    attn_out = _gqa_blocked(q, k, v)
    conv_out = c_b * _conv3(c_c * c_h, conv_w, conv_b)
    mix = jnp.concatenate([attn_out, conv_out], axis=-1) @ w_o
    x = _layernorm(ALPHA * x + mix, ln1_g, ln1_b)
    u = _conv3(x @ w_up, ffn_conv_w, ffn_conv_b)
    gate, val = jnp.split(u, 2, axis=-1)
    ffn = (jax.nn.silu(gate) * val) @ w_down
    return _layernorm(ALPHA * x + ffn, ln2_g, ln2_b)


def _trunk(x, w_in, q_norm, k_norm, conv_w, conv_b, w_o, ln1_g, ln1_b,
           w_up, ffn_conv_w, ffn_conv_b, w_down, ln2_g, ln2_b):
    cos, sin = _axial_rope_tables(x.shape[1])
    for l in range(DEPTH):
        x = _layer(x, cos, sin, w_in[l], q_norm[l], k_norm[l], conv_w[l], conv_b[l], w_o[l],
                   ln1_g[l], ln1_b[l], w_up[l], ffn_conv_w[l], ffn_conv_b[l], w_down[l],
                   ln2_g[l], ln2_b[l])
    return x


def setup_inputs(seed: int = 0) -> dict:
    key = jax.random.key(seed)
    ks = jax.random.split(key, 16)
    f32 = jnp.float32
    nrm = lambda k, shape: jax.random.normal(k, shape, dtype=f32)
    return {
        "x_prompt": nrm(ks[0], (BATCH, SEQ, D_MODEL)),
        "x_sample": nrm(ks[1], (DEC_BATCH, DEC_SEQ, D_MODEL)),
        "w_in": nrm(ks[2], (DEPTH, D_MODEL, IN_W)) * D_MODEL ** -0.5,
        "q_norm": 1.0 + 0.02 * nrm(ks[3], (DEPTH, HEAD_DIM)),
        "k_norm": 1.0 + 0.02 * nrm(ks[4], (DEPTH, HEAD_DIM)),
        "conv_w": nrm(ks[5], (DEPTH, CONV_K, CONV_W)) * CONV_K ** -0.5,
        "conv_b": 0.02 * nrm(ks[6], (DEPTH, CONV_W)),
        "w_o": nrm(ks[7], (DEPTH, MIX_W, D_MODEL)) * (MIX_W ** -0.5 * BETA),
        "ln1_g": 1.0 + 0.02 * nrm(ks[8], (DEPTH, D_MODEL)),
        "ln1_b": 0.02 * nrm(ks[9], (DEPTH, D_MODEL)),
        "w_up": nrm(ks[10], (DEPTH, D_MODEL, 2 * D_FF)) * D_MODEL ** -0.5,
        "ffn_conv_w": nrm(ks[11], (DEPTH, CONV_K, 2 * D_FF)) * CONV_K ** -0.5,
        "ffn_conv_b": 0.02 * nrm(ks[12], (DEPTH, 2 * D_FF)),
        "w_down": nrm(ks[13], (DEPTH, D_FF, D_MODEL)) * (D_FF ** -0.5 * BETA),
        "ln2_g": 1.0 + 0.02 * nrm(ks[14], (DEPTH, D_MODEL)),
        "ln2_b": 0.02 * nrm(ks[15], (DEPTH, D_MODEL)),
    }


def reference(x_prompt, x_sample, w_in, q_norm, k_norm, conv_w, conv_b, w_o, ln1_g, ln1_b,
              w_up, ffn_conv_w, ffn_conv_b, w_down, ln2_g, ln2_b):
    y_prompt = _trunk(x_prompt, w_in, q_norm, k_norm, conv_w, conv_b, w_o, ln1_g, ln1_b,
                      w_up, ffn_conv_w, ffn_conv_b, w_down, ln2_g, ln2_b)
    y_sample = _trunk(x_sample, w_in, q_norm, k_norm, conv_w, conv_b, w_o, ln1_g, ln1_b,
                      w_up, ffn_conv_w, ffn_conv_b, w_down, ln2_g, ln2_b)
    return (y_prompt, y_sample)
```

```python
import math
from contextlib import ExitStack

import numpy as np
import concourse.bass as bass
import concourse.mybir as mybir
from concourse.bass_utils import run_bass_kernel_spmd

F32 = mybir.dt.float32
BF16 = mybir.dt.bfloat16
AF = mybir.ActivationFunctionType
ALU = mybir.AluOpType

D = 1024
DFF = 2816
NJ = 22
NL = 4
ALPHA = (2.0 * NL) ** 0.25
EPS = 1e-6
NV = 228
NEG = -30000.0
NP = 4
NW = 3


class Sem:
    def __init__(self, h, name):
        self.h = h
        self.name = name
        self.n = 0


class Buf:
    def __init__(self, name, excl=False):
        self.name = name
        self.w = None
        self.rs = {}
        self.dsem = None
        self.excl = excl


class Prog:
    ENGS = ("tensor", "vector", "scalar", "gpsimd", "sync")

    def __init__(self, nc, es):
        self.nc = nc
        self.es = es
        self.nsem = 0
        self.sems = []
        self.esem = {}
        self.waited = {e: {} for e in self.ENGS}
        self.epoch = 0
        self.new_epoch()

    def new_sem(self, name):
        s = Sem(self.es.enter_context(self.nc.semaphore(f"{name}_{self.nsem}")), f"{name}_{self.nsem}")
        self.nsem += 1
        self.sems.append(s)
        return s

    def new_epoch(self):
        for e in self.ENGS:
            self.esem[e] = self.new_sem(e[:2] + str(self.epoch))
        self.epoch += 1

    def _wait(self, eng, toks):
        e = getattr(self.nc, eng)
        wd = self.waited[eng]
        best = {}
        for (s, v) in toks:
            if eng == "tensor" and s is self.esem["tensor"]:
                continue
            if wd.get(s.name, 0) >= v:
                continue
            if best.get(s.name, (None, 0))[1] < v:
                best[s.name] = (s, v)
        for (s, v) in best.values():
            assert v <= s.n
            wd[s.name] = v
            e.wait_ge(s.h, v)

    def op(self, eng, fn, reads=(), writes=(), dma=False):
        toks = []
        for b in reads:
            if b.w is not None:
                toks.append(b.w)
            if b.excl:
                toks.extend(t for t in b.rs.values() if t[0] is not self.esem[eng])
        for b in writes:
            if b.w is not None:
                toks.append(b.w)
            toks.extend(b.rs.values())
        self._wait(eng, toks)
        ins = fn(getattr(self.nc, eng))
        if dma:
            b0 = writes[0] if writes else reads[0]
            if b0.dsem is None:
                b0.dsem = {}
            if eng not in b0.dsem:
                b0.dsem[eng] = self.new_sem("d")
            s = b0.dsem[eng]
            s.n += 16
            ins.then_inc(s.h, 16)
        else:
            s = self.esem[eng]
            s.n += 1
            ins.then_inc(s.h, 1)
        assert s.n < 32000, s.name
        tok = (s, s.n)
        for b in reads:
            b.rs[s.name] = tok
        for b in writes:
            b.w = tok
            b.rs = {}
        return tok

    def barrier(self):
        toks = [(s, s.n) for s in self.sems if s.n > 0]
        for e in self.ENGS:
            self._wait(e, toks)


def build(S=8192, L=NL, stop=99):
    NT2 = S // 512
    NT3 = S // 1024
    NKB = S // 128
    HALF2 = NT2 // 2
    HALF3 = NT3 // 2

    nc = bass.Bass("TRN2", target_bir_lowering=False)
    din = lambda n, sh: nc.dram_tensor(n, sh, F32, kind="ExternalInput").ap()
    xT = din("xT", [D, S])
    wqc = din("wqc", [L, D, 2560])
    wkv = din("wkv", [L, D, 384])
    wo = din("wo", [L, D, D])
    wup = din("wup", [L, NJ * 128, 2048])
    wdn = din("wdn", [L, D, DFF])
    vec = din("vec", [128, L, NV])
    cst = din("cst", [128, 544])
    cfg = din("cfg", [128, 8])
    ropeC = din("ropeC", [128, S])
    ropeS = din("ropeS", [128, S])
    yT = nc.dram_tensor("yT", [D, S], F32, kind="ExternalOutput").ap()
    xsA = nc.dram_tensor("xsA", [D, S], F32).ap()
    xsB = nc.dram_tensor("xsB", [D, S], F32).ap()
    wqc_b = nc.dram_tensor("wqc_b", [L, D, 2560], BF16).ap()
    wkv_b = nc.dram_tensor("wkv_b", [L, D, 384], BF16).ap()
    wo_b = nc.dram_tensor("wo_b", [L, D, D], BF16).ap()
    wup_b = nc.dram_tensor("wup_b", [L, NJ * 128, 2048], BF16).ap()
    wdn_b = nc.dram_tensor("wdn_b", [L, D, DFF], BF16).ap()

    fm = lambda ap: ap.rearrange("(c p) t -> p c t", p=128)

    with ExitStack() as es:
        P = Prog(nc, es)
        sb = lambda n, sh, dt: es.enter_context(nc.sbuf_tensor(n, sh, dt))
        ps = es.enter_context(nc.psum_tensor("ps", [128, 4096], F32))
        BK = [Buf(f"bank{i}", excl=True) for i in range(8)]
        bank = lambda i: ps[:, i * 512:(i + 1) * 512]
        class Ring:
            def __init__(self, banks):
                self.banks = list(banks)
                self.i = 0

            def __call__(self):
                b = self.banks[self.i % len(self.banks)]
                self.i += 1
                return b

        nbank = Ring(range(8))

        vec_s = sb("vec_s", [128, L, NV], F32); vec_B = Buf("vec")
        cfg_s = sb("cfg_s", [128, 8], F32); cfg_B = Buf("cfg")
        cstf = sb("cstf", [128, 544], F32); cstf_B = Buf("cstf")
        cstb = sb("cstb", [128, 288], BF16); cstb_B = Buf("cstb")
        KT = sb("KT", [128, S], BF16)
        VS = sb("VS", [128, NKB, 128], BF16)
        KT_B = [Buf(f"KT{i}") for i in range(NT2)]
        VS_B = [Buf(f"VS{i}") for i in range(NT2)]
        bd64 = cstb[:, 0:128]
        onesln = cstb[:, 128:256]
        ones32 = cstb[:, 256:288]
        sel = lambda p: cstf[:, 256 + 128 * p: 256 + 128 * (p + 1)]
        vcol = lambda l, c: vec_s[:, l, c:c + 1]

        sqb = sb("sqb", [128, 512], BF16); sqb_B = Buf("sqb")
        t1 = sb("t1", [128, 512], F32); t1_B = Buf("t1")
        t2 = sb("t2", [128, 512], F32); t2_B = Buf("t2")
        sd = sb("sd", [128, 512], F32); sd_B = Buf("sd")
        rs = sb("rs", [128, 512], F32); rs_B = Buf("rs")
        rC = [sb(f"rC{i}", [128, 512], F32) for i in range(2)]; rC_B = [Buf(f"rC{i}") for i in range(2)]
        rS = [sb(f"rS{i}", [128, 512], F32) for i in range(2)]; rS_B = [Buf(f"rS{i}") for i in range(2)]
        wkv_s = sb("wkv_s", [128, 8, 384], BF16); wkv_B = Buf("wkv_s")
        rope_ctr = [0]
        sdl = sb("sdl", [128, 512], F32); sdl_B = Buf("sdl")
        rsl = sb("rsl", [128, 512], F32); rsl_B = Buf("rsl")
        sd2 = sb("sd2", [128, 512], F32); sd2_B = Buf("sd2")
        rs2 = sb("rs2", [128, 512], F32); rs2_B = Buf("rs2")

        def merge(*gens, head=None):
            gens = list(gens)
            if head is not None:
                gi, n = head
                for _ in range(n):
                    try:
                        next(gens[gi])
                    except StopIteration:
                        gens.pop(gi)
                        break
            while gens:
                for g_ in list(gens):
                    try:
                        next(g_)
                    except StopIteration:
                        gens.remove(g_)

        def run(g_):
            for _ in g_:
                pass

        P.op("sync", lambda e: e.dma_start(out=vec_s[:], in_=vec), writes=[vec_B], dma=True)
        P.op("sync", lambda e: e.dma_start(out=cfg_s[:], in_=cfg), writes=[cfg_B], dma=True)
        P.op("sync", lambda e: e.dma_start(out=cstf[:], in_=cst), writes=[cstf_B], dma=True)
        P.op("vector", lambda e: e.tensor_copy(out=cstb[:, 0:256], in_=cstf[:, 0:256]), reads=[cstf_B], writes=[cstb_B])
        P.op("vector", lambda e: e.tensor_copy(out=cstb[:, 256:288], in_=cstf[:, 512:544]), reads=[cstf_B], writes=[cstb_B])

        P.op("gpsimd", lambda e: e.dma_start(out=wkv_s[:], in_=wkv[0].rearrange("(c p) n -> p c n", p=128)), writes=[wkv_B], dma=True)
        pre2 = ExitStack()
        wqc_s0 = pre2.enter_context(nc.sbuf_tensor("wqc_s_pre", [128, 8, 2560], BF16)); wqc0_B = Buf("wqc_s_pre")
        wo_s0 = pre2.enter_context(nc.sbuf_tensor("wo_s_pre", [128, 8, D], BF16)); wo0_B = Buf("wo_s_pre")
        for k in range(8):
            P.op("gpsimd", lambda e, k=k: e.dma_start(out=wqc_s0[:, k, :], in_=wqc[0, k * 128:(k + 1) * 128, :]), writes=[wqc0_B], dma=True)
        for k in range(8):
            P.op("gpsimd", lambda e, k=k: e.dma_start(out=wo_s0[:, k, :], in_=wo[0, k * 128:(k + 1) * 128, :]), writes=[wo0_B], dma=True)
        WB = {}

        def cast_weight(name, src, dst, l, rows):
            b = Buf(f"{name}{l}")
            WB[(name, l)] = b
            for r0 in range(0, rows, 128):
                r1 = min(rows, r0 + 128)
                P.op("gpsimd", lambda e, r0=r0, r1=r1: e.dma_start(out=dst[l, r0:r1, :], in_=src[l, r0:r1, :]),
                     writes=[b], dma=True)

        for l in range(L):
            if l > 0:
                cast_weight("wqc", wqc, wqc_b, l, D)
                cast_weight("wo", wo, wo_b, l, D)
                cast_weight("wkv", wkv, wkv_b, l, D)
            cast_weight("wup", wup, wup_b, l, NJ * 128)
            cast_weight("wdn", wdn, wdn_b, l, D)

        if stop == 0:
            P.barrier()
            return nc
        def mm(out_ap, pairs, reads, writes):
            def fn(e):
                n = len(pairs)
                ins = None
                for i, (lt, r) in enumerate(pairs):
                    ins = e.matmul(out_ap, lhsT=lt, rhs=r, start=(i == 0), stop=(i == n - 1))
                return ins
            return P.op("tensor", fn, reads=reads, writes=writes)

        def load_rope(a):
            slot = rope_ctr[0] % 2
            rope_ctr[0] += 1
            P.op("sync", lambda e: e.dma_start(out=rC[slot][:], in_=ropeC[:, a:a + 512]), writes=[rC_B[slot]], dma=True)
            P.op("sync", lambda e: e.dma_start(out=rS[slot][:], in_=ropeS[:, a:a + 512]), writes=[rS_B[slot]], dma=True)
            return slot

        def rope_norm_gen(l, bq, bqs, gcol, rslot, out_ap, out_B):
            P.op("scalar", lambda e: e.activation(out=sqb[:], in_=bank(bq), func=AF.Square), reads=[BK[bq]], writes=[sqb_B])
            P.op("vector", lambda e: e.scalar_tensor_tensor(out=t1[:], in0=bank(bq), scalar=vcol(l, gcol), in1=rC[rslot][:],
                                                            op0=ALU.mult, op1=ALU.mult),
                 reads=[BK[bq], rC_B[rslot], vec_B], writes=[t1_B])
            yield
            P.op("vector", lambda e: e.scalar_tensor_tensor(out=t2[:], in0=bank(bqs), scalar=vcol(l, gcol + 1), in1=rS[rslot][:],
                                                            op0=ALU.mult, op1=ALU.mult),
                 reads=[BK[bqs], rS_B[rslot], vec_B], writes=[t2_B])
            bm_ = nbank()
            mm(bank(bm_), [(bd64, sqb[:])], reads=[sqb_B, cstb_B], writes=[BK[bm_]])
            yield
            P.op("scalar", lambda e: e.activation(out=sd[:], in_=bank(bm_), func=AF.Sqrt, bias=EPS, scale=1.0),
                 reads=[BK[bm_]], writes=[sd_B])
            P.op("vector", lambda e: e.tensor_tensor(out=t1[:], in0=t1[:], in1=t2[:], op=ALU.add), reads=[t2_B, t1_B], writes=[t1_B])
            yield
            P.op("vector", lambda e: e.reciprocal(out=rs[:], in_=sd[:]), reads=[sd_B], writes=[rs_B])
            P.op("vector", lambda e: e.tensor_tensor(out=out_ap, in0=t1[:], in1=rs[:], op=ALU.mult), reads=[t1_B, rs_B], writes=[out_B])
            yield

        def rope_norm(*a_):
            run(rope_norm_gen(*a_))

        def kv_tile(l, xk, x_B, i):
            a = 512 * i
            rslot = load_rope(a)
            bk_, bks_ = nbank(), nbank()
            mm(bank(bk_), [(wkv_s[:, k, 0:128], xk(k)) for k in range(8)], reads=[wkv_B, x_B], writes=[BK[bk_]])
            mm(bank(bks_), [(wkv_s[:, k, 128:256], xk(k)) for k in range(8)], reads=[wkv_B, x_B], writes=[BK[bks_]])
            rope_norm(l, bk_, bks_, 2, rslot, KT[:, a:a + 512], KT_B[i])
            bv_ = nbank()

            def fnv(e):
                ins = None
                for tb in range(4):
                    for k in range(8):
                        ins = e.matmul(ps[:, bv_ * 512 + tb * 128: bv_ * 512 + (tb + 1) * 128],
                                       lhsT=xk(k)[:, tb * 128:(tb + 1) * 128], rhs=wkv_s[:, k, 256:384],
                                       start=(k == 0), stop=(k == 7))
                return ins
            P.op("tensor", fnv, reads=[wkv_B, x_B], writes=[BK[bv_]])
            P.op("vector", lambda e: e.tensor_copy(out=VS[:, 4 * i:4 * i + 4, :],
                                                   in_=bank(bv_).rearrange("p (a b) -> p a b", b=128)),
                 reads=[BK[bv_]], writes=[VS_B[i]])

        def load_wkv(l):
            P.op("sync", lambda e: e.dma_start(out=wkv_s[:], in_=wkv_b[l].rearrange("(c p) n -> p c n", p=128)),
                 reads=[WB[("wkv", l)]], writes=[wkv_B], dma=True)

        def layernorm_gen(l, X, x_B, gcol0, bcol0, YB, ybf_B, sdt=None, rst=None, ring=None):
            (sd_, sd_B_) = sdt if sdt else (sdl, sdl_B)
            (rs_, rs_B_) = rst if rst else (rsl, rsl_B)
            yl = ybf_B if isinstance(ybf_B, list) else [ybf_B]
            for c in range(8):
                P.op("vector", lambda e, c=c: e.tensor_copy(out=YB(c), in_=X(c)), reads=[x_B[c]], writes=yl)
                if c % 2:
                    yield
            bm_ = (ring or nbank)()
            mm(bank(bm_), [(onesln, YB(c)) for c in range(8)], reads=yl + [cstb_B], writes=[BK[bm_]])
            yield
            for c in range(8):
                P.op("vector", lambda e, c=c: e.tensor_tensor(out=X(c), in0=X(c), in1=bank(bm_), op=ALU.subtract),
                     reads=[BK[bm_], x_B[c]], writes=[x_B[c]])
                P.op("scalar", lambda e, c=c: e.activation(out=YB(c), in_=X(c), func=AF.Square), reads=[x_B[c]], writes=yl)
                if c % 2:
                    yield
            bv_ = (ring or nbank)()
            mm(bank(bv_), [(onesln, YB(c)) for c in range(8)], reads=yl + [cstb_B], writes=[BK[bv_]])
            yield
            P.op("scalar", lambda e: e.activation(out=sd_[:], in_=bank(bv_), func=AF.Sqrt, bias=EPS, scale=1.0),
                 reads=[BK[bv_]], writes=[sd_B_])
            P.op("vector", lambda e: e.reciprocal(out=rs_[:], in_=sd_[:]), reads=[sd_B_], writes=[rs_B_])
            yield
            for c in range(8):
                P.op("vector", lambda e, c=c: e.tensor_tensor(out=X(c), in0=X(c), in1=rs_[:], op=ALU.mult),
                     reads=[rs_B_, x_B[c]], writes=[x_B[c]])
                P.op("scalar", lambda e, c=c: e.activation(out=X(c), in_=X(c), func=AF.Identity, bias=vcol(l, bcol0 + c),
                                                           scale=vcol(l, gcol0 + c)), reads=[x_B[c], vec_B], writes=[x_B[c]])
                if c % 2:
                    yield

        def layernorm(*a_, **k_):
            run(layernorm_gen(*a_, **k_))

        def load_x_halo(dst, dst_B, src, a, T, first, last, bm_lo, bm_hi):
            lo = 1 if first else 0
            hi = T + 1 if last else T + 2
            P.op("sync", lambda e: e.dma_start(out=dst[:, :, lo:hi], in_=fm(src)[:, :, a - 1 + lo:a - 1 + hi]),
                 writes=[dst_B], dma=True)
            if first:
                P.op("gpsimd", lambda e: e.memset(dst[:, :, 0:1], 0.0), writes=[dst_B])
            if last:
                P.op("gpsimd", lambda e: e.memset(dst[:, :, T + 1:T + 2], 0.0), writes=[dst_B])
            if bm_lo:
                P.op("gpsimd", lambda e: e.tensor_scalar(out=dst[:, :, 0:1], in0=dst[:, :, 0:1], scalar1=cfg_s[:, 0:1],
                                                         scalar2=None, op0=ALU.mult), reads=[cfg_B], writes=[dst_B])
            if bm_hi:
                P.op("gpsimd", lambda e: e.tensor_scalar(out=dst[:, :, T + 1:T + 2], in0=dst[:, :, T + 1:T + 2],
                                                         scalar1=cfg_s[:, 0:1], scalar2=None, op0=ALU.mult),
                     reads=[cfg_B], writes=[dst_B])

        with ExitStack() as s0:
            x0 = [s0.enter_context(nc.sbuf_tensor(f"x0f{i}", [128, 8, 512], F32)) for i in range(2)]
            x0_B = [Buf(f"x0f{i}") for i in range(2)]
            x0b = [s0.enter_context(nc.sbuf_tensor(f"x0b{i}", [128, 8, 512], BF16)) for i in range(2)]
            x0b_B = [Buf(f"x0b{i}") for i in range(2)]
            import os
            KSUB = int(os.environ.get('KSUB', '99'))
            for i in range(NT2):
                sl = i % 2
                if KSUB < 2: continue
                P.op("sync", lambda e: e.dma_start(out=x0[sl][:], in_=fm(xT)[:, :, 512 * i:512 * i + 512]),
                     writes=[x0_B[sl]], dma=True)
                if KSUB < 3: continue
                P.op("vector", lambda e: e.tensor_copy(out=x0b[sl][:], in_=x0[sl][:]), reads=[x0_B[sl]], writes=[x0b_B[sl]])
                if KSUB < 4: continue
                if KSUB == 4:
                    load_rope(512 * i); continue
                kv_tile(0, lambda k: x0b[sl][:, k, :], x0b_B[sl], i)
            P.barrier()
        if stop == 1:
            return nc

        for l in range(L):
            x_src = xT if l == 0 else xsA
            x_dst = yT if l == L - 1 else xsA
            with ExitStack() as s2:
                a2 = lambda n, sh, dt: s2.enter_context(nc.sbuf_tensor(f"{n}_l{l}", sh, dt))
                if l == 0:
                    wqc_s, wqc_B, wo_s, wo_B = wqc_s0, wqc0_B, wo_s0, wo0_B
                else:
                    wqc_s = a2("wqc_s", [128, 8, 2560], BF16); wqc_B = Buf("wqc_s")
                    wo_s = a2("wo_s", [128, 8, D], BF16); wo_B = Buf("wo_s")
                xf = [a2(f"xf{i}", [128, 8, 514], F32) for i in range(2)]; xf_B = [Buf(f"xf{i}") for i in range(2)]
                xb = a2("xb", [128, 8, 514], BF16); xb_B = Buf("xb")
                QT = a2("QT", [128, 4, 512], BF16); QT_B = [Buf(f"QT{c}") for c in range(4)]
                OT = a2("OT", [128, 4, 512], BF16); OT_B = [Buf(f"OT{c}") for c in range(4)]
                CO = [a2(f"CO{q}", [128, 4, 512], BF16) for q in range(2)]; CO_B = [[Buf(f"CO{q}_{c}") for c in range(4)] for q in range(2)]
                pT = [a2(f"pT{i}", [128, 1024], BF16) for i in range(NP)]; pT_B = [Buf(f"pT{i}") for i in range(NP)]
                ct = a2("ct", [128, 512], F32); ct_B = Buf("ct")
                cth = a2("cth", [128, 2], F32); cth_B = Buf("cth")
                gb = a2("gb", [128, 514], F32); gb_B = Buf("gb")
                acc = a2("acc", [128, 512], F32); acc_B = Buf("acc")
                den_s = a2("den_s", [128, 512], F32); den_B = Buf("den_s")
                rden = a2("rden", [128, 512], F32); rden_B = Buf("rden")

                if l > 0:
                    P.op("sync", lambda e: e.dma_start(out=wqc_s[:], in_=wqc_b[l].rearrange("(c p) n -> p c n", p=128)),
                         reads=[WB[("wqc", l)]], writes=[wqc_B], dma=True)
                    P.op("sync", lambda e: e.dma_start(out=wo_s[:], in_=wo_b[l].rearrange("(c p) n -> p c n", p=128)),
                         reads=[WB[("wo", l)]], writes=[wo_B], dma=True)

                XC_B = [[Buf(f"xc{sl_}_{c}") for c in range(8)] for sl_ in range(2)]

                def prefetch(i):
                    sl_ = i % 2
                    load_x_halo(xf[sl_], xf_B[sl_], x_src, 512 * i, 512, i == 0, i == NT2 - 1, i == HALF2, i == HALF2 - 1)
                    P.op("vector", lambda e: e.tensor_copy(out=xb[:], in_=xf[sl_][:]), reads=[xf_B[sl_]], writes=[xb_B])
                    return load_rope(512 * i)

                def pre_gen(i, rslot):
                    xk = lambda k: xb[:, k, 1:513]
                    COi, COi_B = CO[i % 2], CO_B[i % 2]
                    for c in range(4):
                        bq, bqs = nbank(), nbank()
                        mm(bank(bq), [(wqc_s[:, k, c * 128:(c + 1) * 128], xk(k)) for k in range(8)],
                           reads=[wqc_B, xb_B], writes=[BK[bq]])
                        mm(bank(bqs), [(wqc_s[:, k, 512 + c * 128:512 + (c + 1) * 128], xk(k)) for k in range(8)],
                           reads=[wqc_B, xb_B], writes=[BK[bqs]])
                        yield
                        yield from rope_norm_gen(l, bq, bqs, 0, rslot, QT[:, c, :], QT_B[c])
                    for j in range(4):
                        bB, bC, bh, bH = nbank(), nbank(), nbank(), nbank()
                        cB, cC, ch = 1024 + j * 128, 1536 + j * 128, 2048 + j * 128
                        mm(bank(bC), [(wqc_s[:, k, cC:cC + 128], xk(k)) for k in range(8)], reads=[wqc_B, xb_B], writes=[BK[bC]])
                        mm(bank(bh), [(wqc_s[:, k, ch:ch + 128], xk(k)) for k in range(8)], reads=[wqc_B, xb_B], writes=[BK[bh]])
                        yield

                        def fnh(e, cC=cC, ch=ch, bH=bH):
                            ins = None
                            for (col, off) in ((cC, 0), (ch, 2)):
                                for k in range(8):
                                    ins = e.matmul(ps[:, bH * 512 + off: bH * 512 + off + 2],
                                                   lhsT=wqc_s[:, k, col:col + 128],
                                                   rhs=xb[:, k, 0:514:513], start=(k == 0), stop=(k == 7))
                            return ins
                        P.op("tensor", fnh, reads=[wqc_B, xb_B], writes=[BK[bH]])
                        mm(bank(bB), [(wqc_s[:, k, cB:cB + 128], xk(k)) for k in range(8)], reads=[wqc_B, xb_B], writes=[BK[bB]])
                        P.op("scalar", lambda e: e.copy(out=ct[:], in_=bank(bC)), reads=[BK[bC]], writes=[ct_B])
                        yield
                        P.op("vector", lambda e: e.tensor_tensor(out=gb[:, 1:513], in0=ct[:], in1=bank(bh), op=ALU.mult),
                             reads=[ct_B, BK[bh]], writes=[gb_B])
                        P.op("scalar", lambda e: e.copy(out=cth[:], in_=ps[:, bH * 512:bH * 512 + 2]), reads=[BK[bH]], writes=[cth_B])
                        yield
                        P.op("vector", lambda e: e.tensor_tensor(out=gb[:, 0:1], in0=cth[:, 0:1], in1=ps[:, bH * 512 + 2:bH * 512 + 3],
                                                                 op=ALU.mult), reads=[cth_B, BK[bH]], writes=[gb_B])
                        P.op("vector", lambda e: e.tensor_tensor(out=gb[:, 513:514], in0=cth[:, 1:2], in1=ps[:, bH * 512 + 3:bH * 512 + 4],
                                                                 op=ALU.mult), reads=[cth_B, BK[bH]], writes=[gb_B])
                        w0, w1, w2, bb = 4 + 3 * j, 5 + 3 * j, 6 + 3 * j, 16 + j
                        P.op("vector", lambda e: e.tensor_scalar(out=acc[:], in0=gb[:, 1:513], scalar1=vcol(l, w1), scalar2=vcol(l, bb),
                                                                 op0=ALU.mult, op1=ALU.add), reads=[gb_B, vec_B], writes=[acc_B])
                        yield
                        P.op("vector", lambda e: e.scalar_tensor_tensor(out=acc[:], in0=gb[:, 0:512], scalar=vcol(l, w0), in1=acc[:],
                                                                        op0=ALU.mult, op1=ALU.add), reads=[gb_B, vec_B, acc_B], writes=[acc_B])
                        P.op("vector", lambda e: e.scalar_tensor_tensor(out=acc[:], in0=gb[:, 2:514], scalar=vcol(l, w2), in1=acc[:],
                                                                        op0=ALU.mult, op1=ALU.add), reads=[gb_B, vec_B, acc_B], writes=[acc_B])
                        yield
                        P.op("vector", lambda e: e.tensor_tensor(out=COi[:, j, :], in0=acc[:], in1=bank(bB), op=ALU.mult),
                             reads=[acc_B, BK[bB]], writes=[COi_B[j]])
                        yield

                def post_gen(i):
                    a = 512 * i
                    X, X_B, xc_B = xf[i % 2], xf_B[i % 2], XC_B[i % 2]
                    COi, COi_B = CO[i % 2], CO_B[i % 2]
                    for oc in range(8):
                        bo = nbank()
                        pairs = [(wo_s[:, kc, oc * 128:(oc + 1) * 128], (OT[:, kc, :] if kc < 4 else COi[:, kc - 4, :])) for kc in range(8)]
                        mm(bank(bo), pairs, reads=[wo_B] + OT_B + COi_B, writes=[BK[bo]])
                        P.op("vector", lambda e, oc=oc, bo=bo: e.scalar_tensor_tensor(out=X[:, oc, 1:513], in0=X[:, oc, 1:513], scalar=ALPHA,
                                                                                      in1=bank(bo), op0=ALU.mult, op1=ALU.add),
                             reads=[BK[bo], X_B], writes=[xc_B[oc]])
                        yield
                    yield from layernorm_gen(l, lambda c: X[:, c, 1:513], xc_B, 20, 28,
                                             lambda c: pT[c // 2][:, 512 * (c % 2):512 * (c % 2) + 512], pT_B)
                    P.op("sync", lambda e: e.dma_start(out=fm(xsB)[:, :, a:a + 512], in_=X[:, :, 1:513]), reads=[X_B] + xc_B, dma=True)
                    yield

                rslot_cur = prefetch(0)
                run(pre_gen(0, rslot_cur))
                for i in range(NT2):
                    a = 512 * i
                    if i + 1 < NT2:
                        rslot_next = prefetch(i + 1)
                    qh = 1 if i >= HALF2 else 0
                    pending_norm = [None]
                    for g in range(2):
                        nsteps = 2 * NKB

                        def QK(s):
                            kb, p = divmod(s, 2)
                            c = 2 * g + p
                            b0 = 2 * (s % 2)

                            def fn(e):
                                e.matmul(bank(b0), lhsT=KT[0:64, kb * 128:(kb + 1) * 128], rhs=QT[0:64, c, :], start=True, stop=True)
                                return e.matmul(bank(b0 + 1), lhsT=KT[64:128, kb * 128:(kb + 1) * 128], rhs=QT[64:128, c, :],
                                                start=True, stop=True)
                            P.op("tensor", fn, reads=[KT_B[kb // 4], QT_B[c]], writes=[BK[b0], BK[b0 + 1]])

                        def EXP(s):
                            kb, p = divmod(s, 2)
                            b0 = 2 * (s % 2)
                            kh = 1 if kb >= NKB // 2 else 0
                            bcol = 1 + 2 * qh + kh
                            P.op("scalar", lambda e: e.activation(out=pT[s % NP][:], in_=ps[:, b0 * 512:(b0 + 2) * 512], func=AF.Exp,
                                                                  bias=cfg_s[:, bcol:bcol + 1], scale=0.125),
                                 reads=[BK[b0], BK[b0 + 1], cfg_B], writes=[pT_B[s % NP]])

                        def PV(s):
                            kb, p = divmod(s, 2)
                            first, last = kb == 0, kb == NKB - 1

                            def fn(e):
                                e.matmul(ps[0:64, (4 + p) * 512:(5 + p) * 512], lhsT=VS[:, kb, 0:64], rhs=pT[s % NP][:, 0:512],
                                         start=first, stop=last)
                                ins = e.matmul(ps[64:128, (4 + p) * 512:(5 + p) * 512], lhsT=VS[:, kb, 64:128], rhs=pT[s % NP][:, 512:1024],
                                               start=first, stop=last)
                                if p == 1:
                                    for pp in range(2):
                                        for j in range(2):
                                            ii = 2 * pp + j
                                            ss = 2 * kb + pp
                                            ins = e.matmul(ps[32 * ii:32 * ii + 32, 6 * 512:7 * 512], lhsT=ones32, rhs=pT[ss % NP][:, 512 * j:512 * j + 512],
                                                           start=first, stop=last, tile_position=(0, 32 * ii))
                                return ins
                            rd = [VS_B[kb // 4], pT_B[s % NP], cstb_B]
                            wr = [BK[4 + p]]
                            if p == 1:
                                rd.append(pT_B[(s - 1) % NP])
                                wr.append(BK[6])
                            P.op("tensor", fn, reads=rd, writes=wr)

                        def att_norm(g=g):
                            P.op("vector", lambda e: e.tensor_copy(out=den_s[:], in_=bank(6)), reads=[BK[6]], writes=[den_B])
                            for p in range(2):
                                mm(bank(7), [(sel(p), den_s[:])], reads=[den_B, cstf_B], writes=[BK[7]])
                                P.op("vector", lambda e: e.reciprocal(out=rden[:], in_=bank(7)), reads=[BK[7]], writes=[rden_B])
                                P.op("vector", lambda e, p=p: e.tensor_tensor(out=OT[:, 2 * g + p, :], in0=bank(4 + p), in1=rden[:], op=ALU.mult),
                                     reads=[BK[4 + p], rden_B], writes=[OT_B[2 * g + p]])

                        QK(0); EXP(0); QK(1); EXP(1)
                        if pending_norm[0] is not None:
                            pending_norm[0]()
                            pending_norm[0] = None
                        for s in range(nsteps):
                            if s + 2 < nsteps:
                                QK(s + 2); EXP(s + 2)
                            PV(s)
                        pending_norm[0] = att_norm
                    pending_norm[0]()
                    pending_norm[0] = None
                    if i + 1 < NT2:
                        merge(post_gen(i), pre_gen(i + 1, rslot_next), head=(1, 8))
                    else:
                        run(post_gen(i))
                P.barrier()
            if l == 0:
                pre2.close()
            if stop == 2:
                return nc

            with ExitStack() as s3:
                a3 = lambda n, sh, dt: s3.enter_context(nc.sbuf_tensor(f"{n}_l{l}", sh, dt))
                xm = a3("xm", [128, 8, 1026], F32); xm_B = Buf("xm")
                xmb = a3("xmb", [128, 8, 1026], BF16); xmb_B = Buf("xmb")
                hT = a3("hT", [128, NJ, 1024], BF16); hT_B = [Buf(f"hT{j}") for j in range(NJ)]
                wup_s = [a3(f"wup_s{i}", [128, 8, 256], BF16) for i in range(NW)]; wup_B = [Buf(f"wup_s{i}") for i in range(NW)]
                wdn_s = [a3(f"wdn_s{i}", [128, NJ, 128], BF16) for i in range(2)]; wdn_B = [Buf(f"wdn_s{i}") for i in range(2)]
                ac = [a3(f"ac{i}", [128, 1024], F32) for i in range(4)]; ac_B = [Buf(f"ac{i}") for i in range(4)]
                xmc_B = [[Buf(f"xmc{h_}_{c}") for c in range(8)] for h_ in range(2)]
                ysc_B = [Buf("ysc0"), Buf("ysc1")]
                if l + 1 < L:
                    load_wkv(l + 1)
                for t in range(NT3):
                    a = 1024 * t
                    load_x_halo(xm, xm_B, xsB, a, 1024, t == 0, t == NT3 - 1, t == HALF3, t == HALF3 - 1)
                    P.op("scalar", lambda e: e.copy(out=xmb[:, 0:4, :], in_=xm[:, 0:4, :]), reads=[xm_B], writes=[xmb_B])
                    P.op("vector", lambda e: e.tensor_copy(out=xmb[:, 4:8, :], in_=xm[:, 4:8, :]), reads=[xm_B], writes=[xmb_B])

                    def load_wup(j):
                        ws = j % NW
                        P.op("sync", lambda e: e.dma_start(out=wup_s[ws][:],
                                                           in_=wup_b[l, j * 128:(j + 1) * 128, :].rearrange("p (c n) -> p c n", c=8)),
                             reads=[WB[("wup", l)]], writes=[wup_B[ws]], dma=True)
                    load_wup(0); load_wup(1)
                    for j in range(NJ):
                        if j + 2 < NJ:
                            load_wup(j + 2)
                        ws = j % NW
                        for part in range(2):
                            slot = (2 * j + part) % 3
                            b0 = 2 * slot
                            hbk = 6 + (2 * j + part) % 2
                            hbB = BK[hbk]
                            hoff = (6 + (2 * j + part) % 2) * 512
                            A = ac[(2 * j + part) % 4]
                            A_B = ac_B[(2 * j + part) % 4]

                            def fnu(e, part=part, b0=b0, ws=ws):
                                ins = None
                                for hf in range(2):
                                    for k in range(8):
                                        ins = e.matmul(bank(b0 + hf), lhsT=wup_s[ws][:, k, part * 128:(part + 1) * 128],
                                                       rhs=xmb[:, k, 1 + 512 * hf:513 + 512 * hf], start=(k == 0), stop=(k == 7))
                                return ins

                            def fnuh(e, part=part, hoff=hoff, ws=ws):
                                ins = None
                                for k in range(8):
                                    ins = e.matmul(ps[:, hoff:hoff + 2], lhsT=wup_s[ws][:, k, part * 128:(part + 1) * 128],
                                                   rhs=xmb[:, k, 0:1026:1025], start=(k == 0), stop=(k == 7))
                                return ins
                            P.op("tensor", fnu, reads=[wup_B[ws], xmb_B], writes=[BK[b0], BK[b0 + 1]])
                            P.op("tensor", fnuh, reads=[wup_B[ws], xmb_B], writes=[hbB])
                            vc = 52 + (2 * j + part) * 4
                            U = ps[:, b0 * 512:(b0 + 2) * 512]
                            P.op("scalar", lambda e, U=U, A=A, vc=vc: e.activation(out=A[:], in_=U, func=AF.Identity, bias=vcol(l, vc + 3),
                                                                                  scale=vcol(l, vc + 1)),
                                 reads=[BK[b0], BK[b0 + 1], vec_B], writes=[A_B])
                            P.op("vector", lambda e, A=A, vc=vc, hoff=hoff: e.scalar_tensor_tensor(out=A[:, 0:1], in0=ps[:, hoff:hoff + 1], scalar=vcol(l, vc),
                                                                                                  in1=A[:, 0:1], op0=ALU.mult, op1=ALU.add),
                                 reads=[hbB, vec_B, A_B], writes=[A_B])
                            P.op("vector", lambda e, A=A, vc=vc, hoff=hoff: e.scalar_tensor_tensor(out=A[:, 1023:1024], in0=ps[:, hoff + 1:hoff + 2],
                                                                                                  scalar=vcol(l, vc + 2), in1=A[:, 1023:1024],
                                                                                                  op0=ALU.mult, op1=ALU.add),
                                 reads=[hbB, vec_B, A_B], writes=[A_B])
                            P.op("vector", lambda e, U=U, A=A, vc=vc: e.scalar_tensor_tensor(out=A[:, 1:1024], in0=U[:, 0:1023], scalar=vcol(l, vc),
                                                                                            in1=A[:, 1:1024], op0=ALU.mult, op1=ALU.add),
                                 reads=[BK[b0], BK[b0 + 1], vec_B, A_B], writes=[A_B])
                            P.op("vector", lambda e, U=U, A=A, vc=vc: e.scalar_tensor_tensor(out=A[:, 0:1023], in0=U[:, 1:1024], scalar=vcol(l, vc + 2),
                                                                                            in1=A[:, 0:1023], op0=ALU.mult, op1=ALU.add),
                                 reads=[BK[b0], BK[b0 + 1], vec_B, A_B], writes=[A_B])
                        Ag, Ag_B = ac[(2 * j) % 4], ac_B[(2 * j) % 4]
                        Av, Av_B = ac[(2 * j + 1) % 4], ac_B[(2 * j + 1) % 4]
                        P.op("scalar", lambda e, Ag=Ag: e.activation(out=Ag[:], in_=Ag[:], func=AF.Silu), reads=[Ag_B], writes=[Ag_B])
                        P.op("gpsimd", lambda e, Ag=Ag, Av=Av, j=j: e.tensor_tensor(out=hT[:, j, :], in0=Ag[:], in1=Av[:], op=ALU.mult),
                             reads=[Ag_B, Av_B], writes=[hT_B[j]])

                    ring_dn = Ring([0, 1, 2, 3, 4, 5])
                    ring_ln = Ring([6, 7])

                    def load_wdn(n):
                        oc = n % 8
                        P.op("sync", lambda e: e.dma_start(out=wdn_s[n % 2][:],
                                                           in_=wdn_b[l, oc * 128:(oc + 1) * 128, :].rearrange("p (j n) -> p j n", j=NJ)),
                             reads=[WB[("wdn", l)]], writes=[wdn_B[n % 2]], dma=True)

                    def down_gen(hf):
                        for oc in range(8):
                            n = 8 * hf + oc
                            if n + 1 < 16:
                                load_wdn(n + 1)
                            bo = ring_dn()
                            mm(bank(bo), [(wdn_s[n % 2][:, j, :], hT[:, j, 512 * hf:512 * hf + 512]) for j in range(NJ)],
                               reads=[wdn_B[n % 2]] + hT_B, writes=[BK[bo]])
                            P.op("vector", lambda e, oc=oc, bo=bo, hf=hf: e.scalar_tensor_tensor(
                                out=xm[:, oc, 1 + 512 * hf:513 + 512 * hf], in0=xm[:, oc, 1 + 512 * hf:513 + 512 * hf], scalar=ALPHA,
                                in1=bank(bo), op0=ALU.mult, op1=ALU.add), reads=[BK[bo], xm_B], writes=[xmc_B[hf][oc]])
                            yield
                    load_wdn(0)
                    run(down_gen(0))
                    merge(down_gen(1),
                          layernorm_gen(l, lambda c: xm[:, c, 1:513], xmc_B[0], 36, 44, lambda c: xmb[:, c, 1:513], ysc_B[0], ring=ring_ln))
                    run(layernorm_gen(l, lambda c: xm[:, c, 513:1025], xmc_B[1], 36, 44, lambda c: xmb[:, c, 513:1025], ysc_B[1],
                                      sdt=(sd2, sd2_B), rst=(rs2, rs2_B), ring=ring_ln))
                    allc = xmc_B[0] + xmc_B[1]
                    P.op("sync", lambda e: e.dma_start(out=fm(x_dst)[:, :, a:a + 1024], in_=xm[:, :, 1:1025]), reads=[xm_B] + allc, dma=True)
                    if l + 1 < L:
                        P.op("scalar", lambda e: e.copy(out=xmb[:, 0:4, 1:1025], in_=xm[:, 0:4, 1:1025]), reads=[xm_B] + allc, writes=[xmb_B] + ysc_B)
                        P.op("vector", lambda e: e.tensor_copy(out=xmb[:, 4:8, 1:1025], in_=xm[:, 4:8, 1:1025]), reads=[xm_B] + allc, writes=[xmb_B] + ysc_B)
                        for hf in range(2):
                            kv_tile(l + 1, lambda k, hf=hf: xmb[:, k, 1 + 512 * hf:513 + 512 * hf], xmb_B, 2 * t + hf)
                P.barrier()
            if l + 1 < L and P.esem["tensor"].n > 8000:
                P.new_epoch()
    return nc


def _layout_weights(w_in, q_norm, k_norm, conv_w, conv_b, w_o, ln1_g, ln1_b, w_up, ffn_conv_w, ffn_conv_b, w_down,
                    ln2_g, ln2_b, L):
    f = np.float32
    qperm = np.array([(c if p < 64 else 4 + c) * 64 + (p % 64) for c in range(4) for p in range(128)])
    qperm_sw = np.array([(c if p < 64 else 4 + c) * 64 + ((p % 64) ^ 1) for c in range(4) for p in range(128)])
    kperm_sw = np.array([512 + (p ^ 1) for p in range(128)])
    wqc = np.ascontiguousarray(np.concatenate([w_in[:, :, qperm], w_in[:, :, qperm_sw], w_in[:, :, 768:2304]], axis=2), dtype=f)
    wkv = np.ascontiguousarray(np.concatenate([w_in[:, :, 512:640], w_in[:, :, kperm_sw], w_in[:, :, 640:768]], axis=2), dtype=f)
    wo = np.ascontiguousarray(np.concatenate([w_o[:, qperm, :], w_o[:, 512:, :]], axis=1), dtype=f)
    gv = np.concatenate([w_up[:, :, :DFF].reshape(L, 8, 128, NJ, 128), w_up[:, :, DFF:].reshape(L, 8, 128, NJ, 128)], axis=4)
    wup = np.ascontiguousarray(gv.transpose(0, 3, 2, 1, 4).reshape(L, NJ * 128, 2048), dtype=f)
    wdn = np.ascontiguousarray(w_down.reshape(L, NJ, 128, 8, 128).transpose(0, 3, 2, 1, 4).reshape(L, D, DFF), dtype=f)
    vec = np.zeros((128, L, NV), f)
    p = np.arange(128)
    for l in range(L):
        vec[:, l, 0] = q_norm[l][p % 64]
        vec[:, l, 1] = q_norm[l][(p % 64) ^ 1]
        vec[:, l, 2] = k_norm[l][p % 64]
        vec[:, l, 3] = k_norm[l][(p % 64) ^ 1]
        for j in range(4):
            for tap in range(3):
                vec[:, l, 4 + 3 * j + tap] = conv_w[l, tap, j * 128 + p]
            vec[:, l, 16 + j] = conv_b[l, j * 128 + p]
        for c in range(8):
            vec[:, l, 20 + c] = ln1_g[l, c * 128 + p]
            vec[:, l, 28 + c] = ln1_b[l, c * 128 + p]
            vec[:, l, 36 + c] = ln2_g[l, c * 128 + p]
            vec[:, l, 44 + c] = ln2_b[l, c * 128 + p]
        for j in range(NJ):
            for part in range(2):
                base = 52 + (2 * j + part) * 4
                for tap in range(3):
                    vec[:, l, base + tap] = ffn_conv_w[l, tap, part * DFF + j * 128 + p]
                vec[:, l, base + 3] = ffn_conv_b[l, part * DFF + j * 128 + p]
    return wqc, wkv, wo, wup, wdn, vec


def _consts():
    f = np.float32
    cst = np.zeros((128, 544), f)
    p = np.arange(128)
    cst[:, 0:128] = (p[:, None] // 64 == p[None, :] // 64) / 64.0
    cst[:, 128:256] = 1.0 / 1024.0
    for pr in range(2):
        for m in range(128):
            cst[32 * (2 * pr + m // 64), 256 + 128 * pr + m] = 1.0
    cst[:, 512:544] = 1.0
    return cst


def _rope_tables(seq_len, S):
    f = np.float32
    t = np.arange(S) % seq_len
    row = (t // 64).astype(f)
    col = (t % 64).astype(f)
    inv = (f(10000.0) ** (-np.arange(16, dtype=f) / f(16))).astype(f)
    ang = np.concatenate([row[:, None] * inv, col[:, None] * inv], axis=-1).astype(f)
    cos, sin = np.cos(ang).astype(f), np.sin(ang).astype(f)
    d = np.arange(128) % 64
    C = cos[:, d // 2].T
    Sg = sin[:, d // 2].T * np.where(d % 2 == 0, -1.0, 1.0)[:, None]
    return np.ascontiguousarray(C, dtype=f), np.ascontiguousarray(Sg, dtype=f)


def _core_inputs(x_prompt, x_sample, S):
    maps = []
    nb_s = x_sample.shape[0]
    for c in range(nb_s):
        maps.append((np.ascontiguousarray(x_sample[c].T), S))
    for c in range(x_prompt.shape[0] // 2):
        xx = np.concatenate([x_prompt[2 * c], x_prompt[2 * c + 1]], axis=0)
        maps.append((np.ascontiguousarray(xx.T), S // 2))
    return maps


_NC_CACHE = {}


def _run(x_prompt, x_sample, weights, S, L):
    wqc, wkv, wo, wup, wdn, vec = _layout_weights(*weights, L)
    cst = _consts()
    cores = _core_inputs(x_prompt, x_sample, S)
    in_maps = []
    tabs = {}
    for (xT, seqlen) in cores:
        if seqlen not in tabs:
            tabs[seqlen] = _rope_tables(seqlen, S)
        C, Sg = tabs[seqlen]
        cfg = np.zeros((128, 8), np.float32)
        two = seqlen < S
        cfg[:, 0] = 0.0 if two else 1.0
        cfg[:, 2] = NEG if two else 0.0
        cfg[:, 3] = NEG if two else 0.0
        in_maps.append(dict(xT=xT, wqc=wqc, wkv=wkv, wo=wo, wup=wup, wdn=wdn, vec=vec, cst=cst, cfg=cfg, ropeC=C, ropeS=Sg))
    key = (S, L)
    if key not in _NC_CACHE:
        import os
        _NC_CACHE[key] = build(S, L, int(os.environ.get('KSTOP', '99')))
    nc = _NC_CACHE[key]
    res = run_bass_kernel_spmd(nc, in_maps, core_ids=list(range(len(in_maps))))
    return [np.asarray(r["yT"]) for r in res.results]


def kernel(x_prompt, x_sample, w_in, q_norm, k_norm, conv_w, conv_b, w_o, ln1_g, ln1_b,
           w_up, ffn_conv_w, ffn_conv_b, w_down, ln2_g, ln2_b):
    a = lambda v: np.asarray(v, dtype=np.float32)
    x_prompt, x_sample = a(x_prompt), a(x_sample)
    weights = [a(v) for v in (w_in, q_norm, k_norm, conv_w, conv_b, w_o, ln1_g, ln1_b, w_up, ffn_conv_w, ffn_conv_b,
                              w_down, ln2_g, ln2_b)]
    S = x_sample.shape[1]
    outs = _run(x_prompt, x_sample, weights, S, NL)
    nb_s = x_sample.shape[0]
    y_sample = np.stack([outs[c].T for c in range(nb_s)], axis=0)
    yp = []
    for c in range(x_prompt.shape[0] // 2):
        o = outs[nb_s + c].T
        yp.append(o[:S // 2])
        yp.append(o[S // 2:])
    y_prompt = np.stack(yp, axis=0)
    return (np.ascontiguousarray(y_prompt, dtype=np.float32), np.ascontiguousarray(y_sample, dtype=np.float32))
```

```python
import math
from contextlib import ExitStack

import numpy as np
import concourse.bass as bass
import concourse.mybir as mybir
from concourse.bass_utils import run_bass_kernel_spmd

F32 = mybir.dt.float32
BF16 = mybir.dt.bfloat16
AF = mybir.ActivationFunctionType
ALU = mybir.AluOpType

D = 1024
DFF = 2816
NJ = 22
NL = 4
ALPHA = (2.0 * NL) ** 0.25
EPS = 1e-6
NV = 228
NEG = -30000.0
NP = 4
NW = 3


class Sem:
    def __init__(self, h, name):
        self.h = h
        self.name = name
        self.n = 0


class Buf:
    def __init__(self, name, excl=False):
        self.name = name
        self.w = None
        self.rs = {}
        self.dsem = None
        self.excl = excl


class Prog:
    ENGS = ("tensor", "vector", "scalar", "gpsimd", "sync")

    def __init__(self, nc, es):
        self.nc = nc
        self.es = es
        self.nsem = 0
        self.sems = []
        self.esem = {}
        self.waited = {e: {} for e in self.ENGS}
        self.epoch = 0
        self.bg = set()
        self.new_epoch()

    def new_sem(self, name):
        s = Sem(self.es.enter_context(self.nc.semaphore(f"{name}_{self.nsem}")), f"{name}_{self.nsem}")
        self.nsem += 1
        self.sems.append(s)
        return s

    def new_epoch(self):
        for e in self.ENGS:
            self.esem[e] = self.new_sem(e[:2] + str(self.epoch))
        self.epoch += 1

    def _wait(self, eng, toks):
        e = getattr(self.nc, eng)
        wd = self.waited[eng]
        best = {}
        for (s, v) in toks:
            if eng == "tensor" and s is self.esem["tensor"]:
                continue
            if wd.get(s.name, 0) >= v:
                continue
            if best.get(s.name, (None, 0))[1] < v:
                best[s.name] = (s, v)
        for (s, v) in best.values():
            assert v <= s.n
            wd[s.name] = v
            e.wait_ge(s.h, v)

    def op(self, eng, fn, reads=(), writes=(), dma=False):
        toks = []
        for b in reads:
            if b.w is not None:
                toks.append(b.w)
            if b.excl:
                toks.extend(t for t in b.rs.values() if t[0] is not self.esem[eng])
        for b in writes:
            if b.w is not None:
                toks.append(b.w)
            toks.extend(b.rs.values())
        self._wait(eng, toks)
        ins = fn(getattr(self.nc, eng))
        if dma:
            b0 = writes[0] if writes else reads[0]
            if b0.dsem is None:
                b0.dsem = {}
            if eng not in b0.dsem:
                b0.dsem[eng] = self.new_sem("d")
            s = b0.dsem[eng]
            s.n += 16
            ins.then_inc(s.h, 16)
        else:
            s = self.esem[eng]
            s.n += 1
            ins.then_inc(s.h, 1)
        assert s.n < 32000, s.name
        tok = (s, s.n)
        for b in reads:
            b.rs[s.name] = tok
        for b in writes:
            b.w = tok
            b.rs = {}
        return tok

    def barrier(self):
        toks = [(s, s.n) for s in self.sems if s.n > 0 and s.name not in self.bg]
        for e in self.ENGS:
            self._wait(e, toks)


def build(S=8192, L=NL, stop=99):
    NT2 = S // 512
    NT3 = S // 1024
    NKB = S // 128
    HALF2 = NT2 // 2
    HALF3 = NT3 // 2

    nc = bass.Bass("TRN2", target_bir_lowering=False)
    din = lambda n, sh: nc.dram_tensor(n, sh, F32, kind="ExternalInput").ap()
    xT = din("xT", [D, S])
    wqc = din("wqc", [L, D, 2560])
    wkv = din("wkv", [L, D, 384])
    wo = din("wo", [L, D, D])
    wup = din("wup", [L, NJ * 128, 2048])
    wdn = din("wdn", [L, D, DFF])
    vec = din("vec", [128, L, NV])
    cst = din("cst", [128, 544])
    cfg = din("cfg", [128, 8])
    ropeC = din("ropeC", [128, S])
    ropeS = din("ropeS", [128, S])
    yT = nc.dram_tensor("yT", [D, S], F32, kind="ExternalOutput").ap()
    xsA = nc.dram_tensor("xsA", [D, S], F32).ap()
    xsB = nc.dram_tensor("xsB", [D, S], F32).ap()
    wqc_b = nc.dram_tensor("wqc_b", [L, D, 2560], BF16).ap()
    wkv_b = nc.dram_tensor("wkv_b", [L, D, 384], BF16).ap()
    wo_b = nc.dram_tensor("wo_b", [L, D, D], BF16).ap()
    wup_b = nc.dram_tensor("wup_b", [L, NJ * 128, 2048], BF16).ap()
    wdn_b = nc.dram_tensor("wdn_b", [L, D, DFF], BF16).ap()

    fm = lambda ap: ap.rearrange("(c p) t -> p c t", p=128)

    with ExitStack() as es:
        P = Prog(nc, es)
        sb = lambda n, sh, dt: es.enter_context(nc.sbuf_tensor(n, sh, dt))
        ps = es.enter_context(nc.psum_tensor("ps", [128, 4096], F32))
        BK = [Buf(f"bank{i}", excl=True) for i in range(8)]
        bank = lambda i: ps[:, i * 512:(i + 1) * 512]
        class Ring:
            def __init__(self, banks):
                self.banks = list(banks)
                self.i = 0

            def __call__(self):
                b = self.banks[self.i % len(self.banks)]
                self.i += 1
                return b

        nbank = Ring(range(8))

        vec_s = sb("vec_s", [128, L, NV], F32); vec_B = Buf("vec")
        cfg_s = sb("cfg_s", [128, 8], F32); cfg_B = Buf("cfg")
        cstf = sb("cstf", [128, 544], F32); cstf_B = Buf("cstf")
        cstb = sb("cstb", [128, 288], BF16); cstb_B = Buf("cstb")
        KT = sb("KT", [128, S], BF16)
        VS = sb("VS", [128, NKB, 128], BF16)
        KT_B = [Buf(f"KT{i}") for i in range(NT2)]
        VS_B = [Buf(f"VS{i}") for i in range(NT2)]
        bd64 = cstb[:, 0:128]
        onesln = cstb[:, 128:256]
        ones32 = cstb[:, 256:288]
        sel = lambda p: cstf[:, 256 + 128 * p: 256 + 128 * (p + 1)]
        vcol = lambda l, c: vec_s[:, l, c:c + 1]

        sqb = sb("sqb", [128, 512], BF16); sqb_B = Buf("sqb")
        t1 = sb("t1", [128, 512], F32); t1_B = Buf("t1")
        t2 = sb("t2", [128, 512], F32); t2_B = Buf("t2")
        sd = sb("sd", [128, 512], F32); sd_B = Buf("sd")
        rs = sb("rs", [128, 512], F32); rs_B = Buf("rs")
        rC = [sb(f"rC{i}", [128, 512], F32) for i in range(2)]; rC_B = [Buf(f"rC{i}") for i in range(2)]
        rS = [sb(f"rS{i}", [128, 512], F32) for i in range(2)]; rS_B = [Buf(f"rS{i}") for i in range(2)]
        wkv_s = sb("wkv_s", [128, 8, 384], BF16); wkv_B = Buf("wkv_s")
        rope_ctr = [0]
        sdl = sb("sdl", [128, 512], F32); sdl_B = Buf("sdl")
        rsl = sb("rsl", [128, 512], F32); rsl_B = Buf("rsl")
        sd2 = sb("sd2", [128, 512], F32); sd2_B = Buf("sd2")
        rs2 = sb("rs2", [128, 512], F32); rs2_B = Buf("rs2")

        def merge(*gens, head=None):
            gens = list(gens)
            if head is not None:
                gi, n = head
                for _ in range(n):
                    try:
                        next(gens[gi])
                    except StopIteration:
                        gens.pop(gi)
                        break
            while gens:
                for g_ in list(gens):
                    try:
                        next(g_)
                    except StopIteration:
                        gens.remove(g_)

        def run(g_):
            for _ in g_:
                pass

        P.op("sync", lambda e: e.dma_start(out=vec_s[:], in_=vec), writes=[vec_B], dma=True)
        P.op("sync", lambda e: e.dma_start(out=cfg_s[:], in_=cfg), writes=[cfg_B], dma=True)
        P.op("sync", lambda e: e.dma_start(out=cstf[:], in_=cst), writes=[cstf_B], dma=True)
        P.op("vector", lambda e: e.tensor_copy(out=cstb[:, 0:256], in_=cstf[:, 0:256]), reads=[cstf_B], writes=[cstb_B])
        P.op("vector", lambda e: e.tensor_copy(out=cstb[:, 256:288], in_=cstf[:, 512:544]), reads=[cstf_B], writes=[cstb_B])

        P.op("gpsimd", lambda e: e.dma_start(out=wkv_s[:], in_=wkv[0].rearrange("(c p) n -> p c n", p=128)), writes=[wkv_B], dma=True)
        pre2 = ExitStack()
        wqc_s0 = pre2.enter_context(nc.sbuf_tensor("wqc_s_pre", [128, 8, 2560], BF16)); wqc0_B = Buf("wqc_s_pre")
        wo_s0 = pre2.enter_context(nc.sbuf_tensor("wo_s_pre", [128, 8, D], BF16)); wo0_B = Buf("wo_s_pre")
        WB = {}

        def cast_weight(name, src, dst, l, rows):
            b = Buf(f"{name}{l}")
            WB[(name, l)] = b
            for r0 in range(0, rows, 128):
                r1 = min(rows, r0 + 128)
                P.op("gpsimd", lambda e, r0=r0, r1=r1: e.dma_start(out=dst[l, r0:r1, :], in_=src[l, r0:r1, :]),
                     writes=[b], dma=True)
            P.bg.add(b.dsem["gpsimd"].name)

        for l in range(L):
            if l > 0:
                cast_weight("wqc", wqc, wqc_b, l, D)
                cast_weight("wo", wo, wo_b, l, D)
                cast_weight("wkv", wkv, wkv_b, l, D)
            cast_weight("wup", wup, wup_b, l, NJ * 128)
            cast_weight("wdn", wdn, wdn_b, l, D)

        if stop == 0:
            P.barrier()
            return nc
        def mm(out_ap, pairs, reads, writes):
            def fn(e):
                n = len(pairs)
                ins = None
                for i, (lt, r) in enumerate(pairs):
                    ins = e.matmul(out_ap, lhsT=lt, rhs=r, start=(i == 0), stop=(i == n - 1))
                return ins
            return P.op("tensor", fn, reads=reads, writes=writes)

        def load_rope(a):
            slot = rope_ctr[0] % 2
            rope_ctr[0] += 1
            P.op("sync", lambda e: e.dma_start(out=rC[slot][:], in_=ropeC[:, a:a + 512]), writes=[rC_B[slot]], dma=True)
            P.op("sync", lambda e: e.dma_start(out=rS[slot][:], in_=ropeS[:, a:a + 512]), writes=[rS_B[slot]], dma=True)
            return slot

        def rope_norm_gen(l, bq, bqs, gcol, rslot, out_ap, out_B):
            P.op("scalar", lambda e: e.activation(out=sqb[:], in_=bank(bq), func=AF.Square), reads=[BK[bq]], writes=[sqb_B])
            P.op("vector", lambda e: e.scalar_tensor_tensor(out=t1[:], in0=bank(bq), scalar=vcol(l, gcol), in1=rC[rslot][:],
                                                            op0=ALU.mult, op1=ALU.mult),
                 reads=[BK[bq], rC_B[rslot], vec_B], writes=[t1_B])
            yield
            P.op("vector", lambda e: e.scalar_tensor_tensor(out=t2[:], in0=bank(bqs), scalar=vcol(l, gcol + 1), in1=rS[rslot][:],
                                                            op0=ALU.mult, op1=ALU.mult),
                 reads=[BK[bqs], rS_B[rslot], vec_B], writes=[t2_B])
            bm_ = nbank()
            mm(bank(bm_), [(bd64, sqb[:])], reads=[sqb_B, cstb_B], writes=[BK[bm_]])
            yield
            P.op("scalar", lambda e: e.activation(out=sd[:], in_=bank(bm_), func=AF.Sqrt, bias=EPS, scale=1.0),
                 reads=[BK[bm_]], writes=[sd_B])
            P.op("vector", lambda e: e.tensor_tensor(out=t1[:], in0=t1[:], in1=t2[:], op=ALU.add), reads=[t2_B, t1_B], writes=[t1_B])
            yield
            P.op("vector", lambda e: e.reciprocal(out=rs[:], in_=sd[:]), reads=[sd_B], writes=[rs_B])
            P.op("vector", lambda e: e.tensor_tensor(out=out_ap, in0=t1[:], in1=rs[:], op=ALU.mult), reads=[t1_B, rs_B], writes=[out_B])
            yield

        def rope_norm(*a_):
            run(rope_norm_gen(*a_))

        def kv_tile(l, xk, x_B, i):
            a = 512 * i
            rslot = load_rope(a)
            bk_, bks_ = nbank(), nbank()
            mm(bank(bk_), [(wkv_s[:, k, 0:128], xk(k)) for k in range(8)], reads=[wkv_B, x_B], writes=[BK[bk_]])
            mm(bank(bks_), [(wkv_s[:, k, 128:256], xk(k)) for k in range(8)], reads=[wkv_B, x_B], writes=[BK[bks_]])
            rope_norm(l, bk_, bks_, 2, rslot, KT[:, a:a + 512], KT_B[i])
            bv_ = nbank()

            def fnv(e):
                ins = None
                for tb in range(4):
                    for k in range(8):
                        ins = e.matmul(ps[:, bv_ * 512 + tb * 128: bv_ * 512 + (tb + 1) * 128],
                                       lhsT=xk(k)[:, tb * 128:(tb + 1) * 128], rhs=wkv_s[:, k, 256:384],
                                       start=(k == 0), stop=(k == 7))
                return ins
            P.op("tensor", fnv, reads=[wkv_B, x_B], writes=[BK[bv_]])
            P.op("vector", lambda e: e.tensor_copy(out=VS[:, 4 * i:4 * i + 4, :],
                                                   in_=bank(bv_).rearrange("p (a b) -> p a b", b=128)),
                 reads=[BK[bv_]], writes=[VS_B[i]])

        def load_wkv(l):
            P.op("sync", lambda e: e.dma_start(out=wkv_s[:], in_=wkv_b[l].rearrange("(c p) n -> p c n", p=128)),
                 reads=[WB[("wkv", l)]], writes=[wkv_B], dma=True)

        def layernorm_gen(l, X, x_B, gcol0, bcol0, YB, ybf_B, sdt=None, rst=None, ring=None):
            (sd_, sd_B_) = sdt if sdt else (sdl, sdl_B)
            (rs_, rs_B_) = rst if rst else (rsl, rsl_B)
            yl = ybf_B if isinstance(ybf_B, list) else [ybf_B]
            for c in range(8):
                P.op("vector", lambda e, c=c: e.tensor_copy(out=YB(c), in_=X(c)), reads=[x_B[c]], writes=yl)
                if c % 2:
                    yield
            bm_ = (ring or nbank)()
            mm(bank(bm_), [(onesln, YB(c)) for c in range(8)], reads=yl + [cstb_B], writes=[BK[bm_]])
            yield
            for c in range(8):
                P.op("vector", lambda e, c=c: e.tensor_tensor(out=X(c), in0=X(c), in1=bank(bm_), op=ALU.subtract),
                     reads=[BK[bm_], x_B[c]], writes=[x_B[c]])
                P.op("scalar", lambda e, c=c: e.activation(out=YB(c), in_=X(c), func=AF.Square), reads=[x_B[c]], writes=yl)
                if c % 2:
                    yield
            bv_ = (ring or nbank)()
            mm(bank(bv_), [(onesln, YB(c)) for c in range(8)], reads=yl + [cstb_B], writes=[BK[bv_]])
            yield
            P.op("scalar", lambda e: e.activation(out=sd_[:], in_=bank(bv_), func=AF.Sqrt, bias=EPS, scale=1.0),
                 reads=[BK[bv_]], writes=[sd_B_])
            P.op("vector", lambda e: e.reciprocal(out=rs_[:], in_=sd_[:]), reads=[sd_B_], writes=[rs_B_])
            yield
            for c in range(8):
                P.op("vector", lambda e, c=c: e.tensor_tensor(out=X(c), in0=X(c), in1=rs_[:], op=ALU.mult),
                     reads=[rs_B_, x_B[c]], writes=[x_B[c]])
                P.op("scalar", lambda e, c=c: e.activation(out=X(c), in_=X(c), func=AF.Identity, bias=vcol(l, bcol0 + c),
                                                           scale=vcol(l, gcol0 + c)), reads=[x_B[c], vec_B], writes=[x_B[c]])
                if c % 2:
                    yield

        def layernorm(*a_, **k_):
            run(layernorm_gen(*a_, **k_))

        def load_x_halo(dst, dst_B, src, a, T, first, last, bm_lo, bm_hi):
            lo = 1 if first else 0
            hi = T + 1 if last else T + 2
            P.op("sync", lambda e: e.dma_start(out=dst[:, :, lo:hi], in_=fm(src)[:, :, a - 1 + lo:a - 1 + hi]),
                 writes=[dst_B], dma=True)
            if first:
                P.op("gpsimd", lambda e: e.memset(dst[:, :, 0:1], 0.0), writes=[dst_B])
            if last:
                P.op("gpsimd", lambda e: e.memset(dst[:, :, T + 1:T + 2], 0.0), writes=[dst_B])
            if bm_lo:
                P.op("gpsimd", lambda e: e.tensor_scalar(out=dst[:, :, 0:1], in0=dst[:, :, 0:1], scalar1=cfg_s[:, 0:1],
                                                         scalar2=None, op0=ALU.mult), reads=[cfg_B], writes=[dst_B])
            if bm_hi:
                P.op("gpsimd", lambda e: e.tensor_scalar(out=dst[:, :, T + 1:T + 2], in0=dst[:, :, T + 1:T + 2],
                                                         scalar1=cfg_s[:, 0:1], scalar2=None, op0=ALU.mult),
                     reads=[cfg_B], writes=[dst_B])

        with ExitStack() as s0:
            x0 = [s0.enter_context(nc.sbuf_tensor(f"x0f{i}", [128, 8, 512], F32)) for i in range(2)]
            x0_B = [Buf(f"x0f{i}") for i in range(2)]
            x0b = [s0.enter_context(nc.sbuf_tensor(f"x0b{i}", [128, 8, 512], BF16)) for i in range(2)]
            x0b_B = [Buf(f"x0b{i}") for i in range(2)]
            import os
            KSUB = int(os.environ.get('KSUB', '99'))
            stg = [s0.enter_context(nc.sbuf_tensor(f"stg{q}", [128, 2560], F32)) for q in range(2)]
            stg_B = [Buf(f"stg{q}") for q in range(2)]
            wjobs = [(wqc, wqc_s0, wqc0_B, 2560, k) for k in range(8)] + [(wo, wo_s0, wo0_B, D, k) for k in range(8)]

            def stage_weight(n):
                src_, dst_, dB_, ncol, k = wjobs[n]
                q = n % 2
                P.op("sync", lambda e: e.dma_start(out=stg[q][:, 0:ncol], in_=src_[0, k * 128:(k + 1) * 128, :]), writes=[stg_B[q]], dma=True)
                P.op("scalar", lambda e: e.copy(out=dst_[:, k, :], in_=stg[q][:, 0:ncol]), reads=[stg_B[q]], writes=[dB_])
            for i in range(NT2):
                sl = i % 2
                for n in range(len(wjobs) * i // NT2, len(wjobs) * (i + 1) // NT2):
                    stage_weight(n)
                if KSUB < 2: continue
                P.op("sync", lambda e: e.dma_start(out=x0[sl][:], in_=fm(xT)[:, :, 512 * i:512 * i + 512]),
                     writes=[x0_B[sl]], dma=True)
                if KSUB < 3: continue
                P.op("vector", lambda e: e.tensor_copy(out=x0b[sl][:], in_=x0[sl][:]), reads=[x0_B[sl]], writes=[x0b_B[sl]])
                if KSUB < 4: continue
                if KSUB == 4:
                    load_rope(512 * i); continue
                kv_tile(0, lambda k: x0b[sl][:, k, :], x0b_B[sl], i)
            P.barrier()
        if stop == 1:
            return nc

        for l in range(L):
            x_src = xT if l == 0 else xsA
            x_dst = yT if l == L - 1 else xsA
            with ExitStack() as s2:
                a2 = lambda n, sh, dt: s2.enter_context(nc.sbuf_tensor(f"{n}_l{l}", sh, dt))
                if l == 0:
                    wqc_s, wqc_B, wo_s, wo_B = wqc_s0, wqc0_B, wo_s0, wo0_B
                else:
                    wqc_s = a2("wqc_s", [128, 8, 2560], BF16); wqc_B = Buf("wqc_s")
                    wo_s = a2("wo_s", [128, 8, D], BF16); wo_B = Buf("wo_s")
                xf = [a2(f"xf{i}", [128, 8, 514], F32) for i in range(2)]; xf_B = [Buf(f"xf{i}") for i in range(2)]
                xb = a2("xb", [128, 8, 514], BF16); xb_B = Buf("xb")
                QT = a2("QT", [128, 4, 512], BF16); QT_B = [Buf(f"QT{c}") for c in range(4)]
                OT = a2("OT", [128, 4, 512], BF16); OT_B = [Buf(f"OT{c}") for c in range(4)]
                CO = [a2(f"CO{q}", [128, 4, 512], BF16) for q in range(2)]; CO_B = [[Buf(f"CO{q}_{c}") for c in range(4)] for q in range(2)]
                pT = [a2(f"pT{i}", [128, 1024], BF16) for i in range(NP)]; pT_B = [Buf(f"pT{i}") for i in range(NP)]
                ct = a2("ct", [128, 512], F32); ct_B = Buf("ct")
                cth = a2("cth", [128, 2], F32); cth_B = Buf("cth")
                gb = a2("gb", [128, 514], F32); gb_B = Buf("gb")
                acc = a2("acc", [128, 512], F32); acc_B = Buf("acc")
                den_s = a2("den_s", [128, 512], F32); den_B = Buf("den_s")
                rden = a2("rden", [128, 512], F32); rden_B = Buf("rden")

                if l > 0:
                    P.op("sync", lambda e: e.dma_start(out=wqc_s[:], in_=wqc_b[l].rearrange("(c p) n -> p c n", p=128)),
                         reads=[WB[("wqc", l)]], writes=[wqc_B], dma=True)
                    P.op("sync", lambda e: e.dma_start(out=wo_s[:], in_=wo_b[l].rearrange("(c p) n -> p c n", p=128)),
                         reads=[WB[("wo", l)]], writes=[wo_B], dma=True)

                XC_B = [[Buf(f"xc{sl_}_{c}") for c in range(8)] for sl_ in range(2)]

                def prefetch(i):
                    sl_ = i % 2
                    load_x_halo(xf[sl_], xf_B[sl_], x_src, 512 * i, 512, i == 0, i == NT2 - 1, i == HALF2, i == HALF2 - 1)
                    P.op("vector", lambda e: e.tensor_copy(out=xb[:], in_=xf[sl_][:]), reads=[xf_B[sl_]], writes=[xb_B])
                    return load_rope(512 * i)

                def pre_gen(i, rslot):
                    xk = lambda k: xb[:, k, 1:513]
                    COi, COi_B = CO[i % 2], CO_B[i % 2]
                    for c in range(4):
                        bq, bqs = nbank(), nbank()
                        mm(bank(bq), [(wqc_s[:, k, c * 128:(c + 1) * 128], xk(k)) for k in range(8)],
                           reads=[wqc_B, xb_B], writes=[BK[bq]])
                        mm(bank(bqs), [(wqc_s[:, k, 512 + c * 128:512 + (c + 1) * 128], xk(k)) for k in range(8)],
                           reads=[wqc_B, xb_B], writes=[BK[bqs]])
                        yield
                        yield from rope_norm_gen(l, bq, bqs, 0, rslot, QT[:, c, :], QT_B[c])
                    for j in range(4):
                        bB, bC, bh, bH = nbank(), nbank(), nbank(), nbank()
                        cB, cC, ch = 1024 + j * 128, 1536 + j * 128, 2048 + j * 128
                        mm(bank(bC), [(wqc_s[:, k, cC:cC + 128], xk(k)) for k in range(8)], reads=[wqc_B, xb_B], writes=[BK[bC]])
                        mm(bank(bh), [(wqc_s[:, k, ch:ch + 128], xk(k)) for k in range(8)], reads=[wqc_B, xb_B], writes=[BK[bh]])
                        yield

                        def fnh(e, cC=cC, ch=ch, bH=bH):
                            ins = None
                            for (col, off) in ((cC, 0), (ch, 2)):
                                for k in range(8):
                                    ins = e.matmul(ps[:, bH * 512 + off: bH * 512 + off + 2],
                                                   lhsT=wqc_s[:, k, col:col + 128],
                                                   rhs=xb[:, k, 0:514:513], start=(k == 0), stop=(k == 7))
                            return ins
                        P.op("tensor", fnh, reads=[wqc_B, xb_B], writes=[BK[bH]])
                        mm(bank(bB), [(wqc_s[:, k, cB:cB + 128], xk(k)) for k in range(8)], reads=[wqc_B, xb_B], writes=[BK[bB]])
                        P.op("scalar", lambda e: e.copy(out=ct[:], in_=bank(bC)), reads=[BK[bC]], writes=[ct_B])
                        yield
                        P.op("vector", lambda e: e.tensor_tensor(out=gb[:, 1:513], in0=ct[:], in1=bank(bh), op=ALU.mult),
                             reads=[ct_B, BK[bh]], writes=[gb_B])
                        P.op("scalar", lambda e: e.copy(out=cth[:], in_=ps[:, bH * 512:bH * 512 + 2]), reads=[BK[bH]], writes=[cth_B])
                        yield
                        P.op("vector", lambda e: e.tensor_tensor(out=gb[:, 0:1], in0=cth[:, 0:1], in1=ps[:, bH * 512 + 2:bH * 512 + 3],
                                                                 op=ALU.mult), reads=[cth_B, BK[bH]], writes=[gb_B])
                        P.op("vector", lambda e: e.tensor_tensor(out=gb[:, 513:514], in0=cth[:, 1:2], in1=ps[:, bH * 512 + 3:bH * 512 + 4],
                                                                 op=ALU.mult), reads=[cth_B, BK[bH]], writes=[gb_B])
                        w0, w1, w2, bb = 4 + 3 * j, 5 + 3 * j, 6 + 3 * j, 16 + j
                        P.op("vector", lambda e: e.tensor_scalar(out=acc[:], in0=gb[:, 1:513], scalar1=vcol(l, w1), scalar2=vcol(l, bb),
                                                                 op0=ALU.mult, op1=ALU.add), reads=[gb_B, vec_B], writes=[acc_B])
                        yield
                        P.op("vector", lambda e: e.scalar_tensor_tensor(out=acc[:], in0=gb[:, 0:512], scalar=vcol(l, w0), in1=acc[:],
                                                                        op0=ALU.mult, op1=ALU.add), reads=[gb_B, vec_B, acc_B], writes=[acc_B])
                        P.op("vector", lambda e: e.scalar_tensor_tensor(out=acc[:], in0=gb[:, 2:514], scalar=vcol(l, w2), in1=acc[:],
                                                                        op0=ALU.mult, op1=ALU.add), reads=[gb_B, vec_B, acc_B], writes=[acc_B])
                        yield
                        P.op("vector", lambda e: e.tensor_tensor(out=COi[:, j, :], in0=acc[:], in1=bank(bB), op=ALU.mult),
                             reads=[acc_B, BK[bB]], writes=[COi_B[j]])
                        yield

                def post_gen(i):
                    a = 512 * i
                    X, X_B, xc_B = xf[i % 2], xf_B[i % 2], XC_B[i % 2]
                    COi, COi_B = CO[i % 2], CO_B[i % 2]
                    for oc in range(8):
                        bo = nbank()
                        pairs = [(wo_s[:, kc, oc * 128:(oc + 1) * 128], (OT[:, kc, :] if kc < 4 else COi[:, kc - 4, :])) for kc in range(8)]
                        mm(bank(bo), pairs, reads=[wo_B] + OT_B + COi_B, writes=[BK[bo]])
                        P.op("vector", lambda e, oc=oc, bo=bo: e.scalar_tensor_tensor(out=X[:, oc, 1:513], in0=X[:, oc, 1:513], scalar=ALPHA,
                                                                                      in1=bank(bo), op0=ALU.mult, op1=ALU.add),
                             reads=[BK[bo], X_B], writes=[xc_B[oc]])
                        yield
                    yield from layernorm_gen(l, lambda c: X[:, c, 1:513], xc_B, 20, 28,
                                             lambda c: pT[c // 2][:, 512 * (c % 2):512 * (c % 2) + 512], pT_B)
                    P.op("sync", lambda e: e.dma_start(out=fm(xsB)[:, :, a:a + 512], in_=X[:, :, 1:513]), reads=[X_B] + xc_B, dma=True)
                    yield

                rslot_cur = prefetch(0)
                run(pre_gen(0, rslot_cur))
                for i in range(NT2):
                    a = 512 * i
                    if i + 1 < NT2:
                        rslot_next = prefetch(i + 1)
                    qh = 1 if i >= HALF2 else 0
                    pending_norm = [None]
                    for g in range(2):
                        nsteps = 2 * NKB

                        def QK(s):
                            kb, p = divmod(s, 2)
                            c = 2 * g + p
                            b0 = 2 * (s % 2)

                            def fn(e):
                                e.matmul(bank(b0), lhsT=KT[0:64, kb * 128:(kb + 1) * 128], rhs=QT[0:64, c, :], start=True, stop=True)
                                return e.matmul(bank(b0 + 1), lhsT=KT[64:128, kb * 128:(kb + 1) * 128], rhs=QT[64:128, c, :],
                                                start=True, stop=True)
                            P.op("tensor", fn, reads=[KT_B[kb // 4], QT_B[c]], writes=[BK[b0], BK[b0 + 1]])

                        def EXP(s):
                            kb, p = divmod(s, 2)
                            b0 = 2 * (s % 2)
                            kh = 1 if kb >= NKB // 2 else 0
                            bcol = 1 + 2 * qh + kh
                            P.op("scalar", lambda e: e.activation(out=pT[s % NP][:], in_=ps[:, b0 * 512:(b0 + 2) * 512], func=AF.Exp,
                                                                  bias=cfg_s[:, bcol:bcol + 1], scale=0.125),
                                 reads=[BK[b0], BK[b0 + 1], cfg_B], writes=[pT_B[s % NP]])

                        def PV(s):
                            kb, p = divmod(s, 2)
                            first, last = kb == 0, kb == NKB - 1

                            def fn(e):
                                e.matmul(ps[0:64, (4 + p) * 512:(5 + p) * 512], lhsT=VS[:, kb, 0:64], rhs=pT[s % NP][:, 0:512],
                                         start=first, stop=last)
                                ins = e.matmul(ps[64:128, (4 + p) * 512:(5 + p) * 512], lhsT=VS[:, kb, 64:128], rhs=pT[s % NP][:, 512:1024],
                                               start=first, stop=last)
                                if p == 1:
                                    for pp in range(2):
                                        for j in range(2):
                                            ii = 2 * pp + j
                                            ss = 2 * kb + pp
                                            ins = e.matmul(ps[32 * ii:32 * ii + 32, 6 * 512:7 * 512], lhsT=ones32, rhs=pT[ss % NP][:, 512 * j:512 * j + 512],
                                                           start=first, stop=last, tile_position=(0, 32 * ii))
                                return ins
                            rd = [VS_B[kb // 4], pT_B[s % NP], cstb_B]
                            wr = [BK[4 + p]]
                            if p == 1:
                                rd.append(pT_B[(s - 1) % NP])
                                wr.append(BK[6])
                            P.op("tensor", fn, reads=rd, writes=wr)

                        def att_norm(g=g):
                            P.op("vector", lambda e: e.tensor_copy(out=den_s[:], in_=bank(6)), reads=[BK[6]], writes=[den_B])
                            for p in range(2):
                                mm(bank(7), [(sel(p), den_s[:])], reads=[den_B, cstf_B], writes=[BK[7]])
                                P.op("vector", lambda e: e.reciprocal(out=rden[:], in_=bank(7)), reads=[BK[7]], writes=[rden_B])
                                P.op("vector", lambda e, p=p: e.tensor_tensor(out=OT[:, 2 * g + p, :], in0=bank(4 + p), in1=rden[:], op=ALU.mult),
                                     reads=[BK[4 + p], rden_B], writes=[OT_B[2 * g + p]])

                        QK(0); EXP(0); QK(1); EXP(1)
                        if pending_norm[0] is not None:
                            pending_norm[0]()
                            pending_norm[0] = None
                        for s in range(nsteps):
                            if s + 2 < nsteps:
                                QK(s + 2); EXP(s + 2)
                            PV(s)
                        pending_norm[0] = att_norm
                    pending_norm[0]()
                    pending_norm[0] = None
                    if i + 1 < NT2:
                        merge(post_gen(i), pre_gen(i + 1, rslot_next), head=(1, 8))
                    else:
                        run(post_gen(i))
                P.barrier()
            if l == 0:
                pre2.close()
            if stop == 2:
                return nc

            with ExitStack() as s3:
                a3 = lambda n, sh, dt: s3.enter_context(nc.sbuf_tensor(f"{n}_l{l}", sh, dt))
                xm = a3("xm", [128, 8, 1026], F32); xm_B = Buf("xm")
                xmb = a3("xmb", [128, 8, 1026], BF16); xmb_B = Buf("xmb")
                hT = a3("hT", [128, NJ, 1024], BF16); hT_B = [Buf(f"hT{j}") for j in range(NJ)]
                wup_s = [a3(f"wup_s{i}", [128, 8, 256], BF16) for i in range(NW)]; wup_B = [Buf(f"wup_s{i}") for i in range(NW)]
                wdn_s = [a3(f"wdn_s{i}", [128, NJ, 128], BF16) for i in range(2)]; wdn_B = [Buf(f"wdn_s{i}") for i in range(2)]
                ac = [a3(f"ac{i}", [128, 1024], F32) for i in range(4)]; ac_B = [Buf(f"ac{i}") for i in range(4)]
                xmc_B = [[Buf(f"xmc{h_}_{c}") for c in range(8)] for h_ in range(2)]
                ysc_B = [Buf("ysc0"), Buf("ysc1")]
                if l + 1 < L:
                    load_wkv(l + 1)
                for t in range(NT3):
                    a = 1024 * t
                    load_x_halo(xm, xm_B, xsB, a, 1024, t == 0, t == NT3 - 1, t == HALF3, t == HALF3 - 1)
                    P.op("scalar", lambda e: e.copy(out=xmb[:, 0:4, :], in_=xm[:, 0:4, :]), reads=[xm_B], writes=[xmb_B])
                    P.op("vector", lambda e: e.tensor_copy(out=xmb[:, 4:8, :], in_=xm[:, 4:8, :]), reads=[xm_B], writes=[xmb_B])

                    def load_wup(j):
                        ws = j % NW
                        P.op("sync", lambda e: e.dma_start(out=wup_s[ws][:],
                                                           in_=wup_b[l, j * 128:(j + 1) * 128, :].rearrange("p (c n) -> p c n", c=8)),
                             reads=[WB[("wup", l)]], writes=[wup_B[ws]], dma=True)
                    load_wup(0); load_wup(1)
                    for j in range(NJ):
                        if j + 2 < NJ:
                            load_wup(j + 2)
                        ws = j % NW
                        for part in range(2):
                            slot = (2 * j + part) % 3
                            b0 = 2 * slot
                            hbk = 6 + (2 * j + part) % 2
                            hbB = BK[hbk]
                            hoff = (6 + (2 * j + part) % 2) * 512
                            A = ac[(2 * j + part) % 4]
                            A_B = ac_B[(2 * j + part) % 4]

                            def fnu(e, part=part, b0=b0, ws=ws):
                                ins = None
                                for hf in range(2):
                                    for k in range(8):
                                        ins = e.matmul(bank(b0 + hf), lhsT=wup_s[ws][:, k, part * 128:(part + 1) * 128],
                                                       rhs=xmb[:, k, 1 + 512 * hf:513 + 512 * hf], start=(k == 0), stop=(k == 7))
                                return ins

                            def fnuh(e, part=part, hoff=hoff, ws=ws):
                                ins = None
                                for k in range(8):
                                    ins = e.matmul(ps[:, hoff:hoff + 2], lhsT=wup_s[ws][:, k, part * 128:(part + 1) * 128],
                                                   rhs=xmb[:, k, 0:1026:1025], start=(k == 0), stop=(k == 7))
                                return ins
                            P.op("tensor", fnu, reads=[wup_B[ws], xmb_B], writes=[BK[b0], BK[b0 + 1]])
                            P.op("tensor", fnuh, reads=[wup_B[ws], xmb_B], writes=[hbB])
                            vc = 52 + (2 * j + part) * 4
                            U = ps[:, b0 * 512:(b0 + 2) * 512]
                            P.op("scalar", lambda e, U=U, A=A, vc=vc: e.activation(out=A[:], in_=U, func=AF.Identity, bias=vcol(l, vc + 3),
                                                                                  scale=vcol(l, vc + 1)),
                                 reads=[BK[b0], BK[b0 + 1], vec_B], writes=[A_B])
                            P.op("vector", lambda e, A=A, vc=vc, hoff=hoff: e.scalar_tensor_tensor(out=A[:, 0:1], in0=ps[:, hoff:hoff + 1], scalar=vcol(l, vc),
                                                                                                  in1=A[:, 0:1], op0=ALU.mult, op1=ALU.add),
                                 reads=[hbB, vec_B, A_B], writes=[A_B])
                            P.op("vector", lambda e, A=A, vc=vc, hoff=hoff: e.scalar_tensor_tensor(out=A[:, 1023:1024], in0=ps[:, hoff + 1:hoff + 2],
                                                                                                  scalar=vcol(l, vc + 2), in1=A[:, 1023:1024],
                                                                                                  op0=ALU.mult, op1=ALU.add),
                                 reads=[hbB, vec_B, A_B], writes=[A_B])
                            P.op("vector", lambda e, U=U, A=A, vc=vc: e.scalar_tensor_tensor(out=A[:, 1:1024], in0=U[:, 0:1023], scalar=vcol(l, vc),
                                                                                            in1=A[:, 1:1024], op0=ALU.mult, op1=ALU.add),
                                 reads=[BK[b0], BK[b0 + 1], vec_B, A_B], writes=[A_B])
                            P.op("vector", lambda e, U=U, A=A, vc=vc: e.scalar_tensor_tensor(out=A[:, 0:1023], in0=U[:, 1:1024], scalar=vcol(l, vc + 2),
                                                                                            in1=A[:, 0:1023], op0=ALU.mult, op1=ALU.add),
                                 reads=[BK[b0], BK[b0 + 1], vec_B, A_B], writes=[A_B])
                        Ag, Ag_B = ac[(2 * j) % 4], ac_B[(2 * j) % 4]
                        Av, Av_B = ac[(2 * j + 1) % 4], ac_B[(2 * j + 1) % 4]
                        P.op("scalar", lambda e, Ag=Ag: e.activation(out=Ag[:], in_=Ag[:], func=AF.Silu), reads=[Ag_B], writes=[Ag_B])
                        P.op("gpsimd", lambda e, Ag=Ag, Av=Av, j=j: e.tensor_tensor(out=hT[:, j, :], in0=Ag[:], in1=Av[:], op=ALU.mult),
                             reads=[Ag_B, Av_B], writes=[hT_B[j]])

                    ring_dn = Ring([0, 1, 2, 3, 4, 5])
                    ring_ln = Ring([6, 7])

                    def load_wdn(n):
                        oc = n % 8
                        P.op("sync", lambda e: e.dma_start(out=wdn_s[n % 2][:],
                                                           in_=wdn_b[l, oc * 128:(oc + 1) * 128, :].rearrange("p (j n) -> p j n", j=NJ)),
                             reads=[WB[("wdn", l)]], writes=[wdn_B[n % 2]], dma=True)

                    def down_gen(hf):
                        for oc in range(8):
                            n = 8 * hf + oc
                            if n + 1 < 16:
                                load_wdn(n + 1)
                            bo = ring_dn()
                            mm(bank(bo), [(wdn_s[n % 2][:, j, :], hT[:, j, 512 * hf:512 * hf + 512]) for j in range(NJ)],
                               reads=[wdn_B[n % 2]] + hT_B, writes=[BK[bo]])
                            P.op("vector", lambda e, oc=oc, bo=bo, hf=hf: e.scalar_tensor_tensor(
                                out=xm[:, oc, 1 + 512 * hf:513 + 512 * hf], in0=xm[:, oc, 1 + 512 * hf:513 + 512 * hf], scalar=ALPHA,
                                in1=bank(bo), op0=ALU.mult, op1=ALU.add), reads=[BK[bo], xm_B], writes=[xmc_B[hf][oc]])
                            yield
                    load_wdn(0)
                    run(down_gen(0))
                    merge(down_gen(1),
                          layernorm_gen(l, lambda c: xm[:, c, 1:513], xmc_B[0], 36, 44, lambda c: xmb[:, c, 1:513], ysc_B[0], ring=ring_ln))
                    run(layernorm_gen(l, lambda c: xm[:, c, 513:1025], xmc_B[1], 36, 44, lambda c: xmb[:, c, 513:1025], ysc_B[1],
                                      sdt=(sd2, sd2_B), rst=(rs2, rs2_B), ring=ring_ln))
                    allc = xmc_B[0] + xmc_B[1]
                    P.op("sync", lambda e: e.dma_start(out=fm(x_dst)[:, :, a:a + 1024], in_=xm[:, :, 1:1025]), reads=[xm_B] + allc, dma=True)
                    if l + 1 < L:
                        P.op("scalar", lambda e: e.copy(out=xmb[:, 0:4, 1:1025], in_=xm[:, 0:4, 1:1025]), reads=[xm_B] + allc, writes=[xmb_B] + ysc_B)
                        P.op("vector", lambda e: e.tensor_copy(out=xmb[:, 4:8, 1:1025], in_=xm[:, 4:8, 1:1025]), reads=[xm_B] + allc, writes=[xmb_B] + ysc_B)
                        for hf in range(2):
                            kv_tile(l + 1, lambda k, hf=hf: xmb[:, k, 1 + 512 * hf:513 + 512 * hf], xmb_B, 2 * t + hf)
                P.barrier()
            if l + 1 < L and P.esem["tensor"].n > 8000:
                P.new_epoch()
    return nc


def _layout_weights(w_in, q_norm, k_norm, conv_w, conv_b, w_o, ln1_g, ln1_b, w_up, ffn_conv_w, ffn_conv_b, w_down,
                    ln2_g, ln2_b, L):
    f = np.float32
    qperm = np.array([(c if p < 64 else 4 + c) * 64 + (p % 64) for c in range(4) for p in range(128)])
    qperm_sw = np.array([(c if p < 64 else 4 + c) * 64 + ((p % 64) ^ 1) for c in range(4) for p in range(128)])
    kperm_sw = np.array([512 + (p ^ 1) for p in range(128)])
    wqc = np.ascontiguousarray(np.concatenate([w_in[:, :, qperm], w_in[:, :, qperm_sw], w_in[:, :, 768:2304]], axis=2), dtype=f)
    wkv = np.ascontiguousarray(np.concatenate([w_in[:, :, 512:640], w_in[:, :, kperm_sw], w_in[:, :, 640:768]], axis=2), dtype=f)
    wo = np.ascontiguousarray(np.concatenate([w_o[:, qperm, :], w_o[:, 512:, :]], axis=1), dtype=f)
    gv = np.concatenate([w_up[:, :, :DFF].reshape(L, 8, 128, NJ, 128), w_up[:, :, DFF:].reshape(L, 8, 128, NJ, 128)], axis=4)
    wup = np.ascontiguousarray(gv.transpose(0, 3, 2, 1, 4).reshape(L, NJ * 128, 2048), dtype=f)
    wdn = np.ascontiguousarray(w_down.reshape(L, NJ, 128, 8, 128).transpose(0, 3, 2, 1, 4).reshape(L, D, DFF), dtype=f)
    vec = np.zeros((128, L, NV), f)
    p = np.arange(128)
    for l in range(L):
        vec[:, l, 0] = q_norm[l][p % 64]
        vec[:, l, 1] = q_norm[l][(p % 64) ^ 1]
        vec[:, l, 2] = k_norm[l][p % 64]
        vec[:, l, 3] = k_norm[l][(p % 64) ^ 1]
        for j in range(4):
            for tap in range(3):
                vec[:, l, 4 + 3 * j + tap] = conv_w[l, tap, j * 128 + p]
            vec[:, l, 16 + j] = conv_b[l, j * 128 + p]
        for c in range(8):
            vec[:, l, 20 + c] = ln1_g[l, c * 128 + p]
            vec[:, l, 28 + c] = ln1_b[l, c * 128 + p]
            vec[:, l, 36 + c] = ln2_g[l, c * 128 + p]
            vec[:, l, 44 + c] = ln2_b[l, c * 128 + p]
        for j in range(NJ):
            for part in range(2):
                base = 52 + (2 * j + part) * 4
                for tap in range(3):
                    vec[:, l, base + tap] = ffn_conv_w[l, tap, part * DFF + j * 128 + p]
                vec[:, l, base + 3] = ffn_conv_b[l, part * DFF + j * 128 + p]
    return wqc, wkv, wo, wup, wdn, vec


def _consts():
    f = np.float32
    cst = np.zeros((128, 544), f)
    p = np.arange(128)
    cst[:, 0:128] = (p[:, None] // 64 == p[None, :] // 64) / 64.0
    cst[:, 128:256] = 1.0 / 1024.0
    for pr in range(2):
        for m in range(128):
            cst[32 * (2 * pr + m // 64), 256 + 128 * pr + m] = 1.0
    cst[:, 512:544] = 1.0
    return cst


def _rope_tables(seq_len, S):
    f = np.float32
    t = np.arange(S) % seq_len
    row = (t // 64).astype(f)
    col = (t % 64).astype(f)
    inv = (f(10000.0) ** (-np.arange(16, dtype=f) / f(16))).astype(f)
    ang = np.concatenate([row[:, None] * inv, col[:, None] * inv], axis=-1).astype(f)
    cos, sin = np.cos(ang).astype(f), np.sin(ang).astype(f)
    d = np.arange(128) % 64
    C = cos[:, d // 2].T
    Sg = sin[:, d // 2].T * np.where(d % 2 == 0, -1.0, 1.0)[:, None]
    return np.ascontiguousarray(C, dtype=f), np.ascontiguousarray(Sg, dtype=f)


def _core_inputs(x_prompt, x_sample, S):
    maps = []
    nb_s = x_sample.shape[0]
    for c in range(nb_s):
        maps.append((np.ascontiguousarray(x_sample[c].T), S))
    for c in range(x_prompt.shape[0] // 2):
        xx = np.concatenate([x_prompt[2 * c], x_prompt[2 * c + 1]], axis=0)
        maps.append((np.ascontiguousarray(xx.T), S // 2))
    return maps


_NC_CACHE = {}


def _run(x_prompt, x_sample, weights, S, L):
    wqc, wkv, wo, wup, wdn, vec = _layout_weights(*weights, L)
    cst = _consts()
    cores = _core_inputs(x_prompt, x_sample, S)
    in_maps = []
    tabs = {}
    for (xT, seqlen) in cores:
        if seqlen not in tabs:
            tabs[seqlen] = _rope_tables(seqlen, S)
        C, Sg = tabs[seqlen]
        cfg = np.zeros((128, 8), np.float32)
        two = seqlen < S
        cfg[:, 0] = 0.0 if two else 1.0
        cfg[:, 2] = NEG if two else 0.0
        cfg[:, 3] = NEG if two else 0.0
        in_maps.append(dict(xT=xT, wqc=wqc, wkv=wkv, wo=wo, wup=wup, wdn=wdn, vec=vec, cst=cst, cfg=cfg, ropeC=C, ropeS=Sg))
    key = (S, L)
    if key not in _NC_CACHE:
        import os
        _NC_CACHE[key] = build(S, L, int(os.environ.get('KSTOP', '99')))
    nc = _NC_CACHE[key]
    res = run_bass_kernel_spmd(nc, in_maps, core_ids=list(range(len(in_maps))))
    return [np.asarray(r["yT"]) for r in res.results]


def kernel(x_prompt, x_sample, w_in, q_norm, k_norm, conv_w, conv_b, w_o, ln1_g, ln1_b,
           w_up, ffn_conv_w, ffn_conv_b, w_down, ln2_g, ln2_b):
    a = lambda v: np.asarray(v, dtype=np.float32)
    x_prompt, x_sample = a(x_prompt), a(x_sample)
    weights = [a(v) for v in (w_in, q_norm, k_norm, conv_w, conv_b, w_o, ln1_g, ln1_b, w_up, ffn_conv_w, ffn_conv_b,
                              w_down, ln2_g, ln2_b)]
    S = x_sample.shape[1]
    outs = _run(x_prompt, x_sample, weights, S, NL)
    nb_s = x_sample.shape[0]
    y_sample = np.stack([outs[c].T for c in range(nb_s)], axis=0)
    yp = []
    for c in range(x_prompt.shape[0] // 2):
        o = outs[nb_s + c].T
        yp.append(o[:S // 2])
        yp.append(o[S // 2:])
    y_prompt = np.stack(yp, axis=0)
    return (np.ascontiguousarray(y_prompt, dtype=np.float32), np.ascontiguousarray(y_sample, dtype=np.float32))
```

```python
import math
from contextlib import ExitStack

import numpy as np
import concourse.bass as bass
import concourse.mybir as mybir
from concourse.bass_utils import run_bass_kernel_spmd

F32 = mybir.dt.float32
BF16 = mybir.dt.bfloat16
AF = mybir.ActivationFunctionType
ALU = mybir.AluOpType

D = 1024
DFF = 2816
NJ = 22
NL = 4
ALPHA = (2.0 * NL) ** 0.25
EPS = 1e-6
NV = 228
NEG = -30000.0
NP = 4
NW = 3


class Sem:
    def __init__(self, h, name):
        self.h = h
        self.name = name
        self.n = 0


class Buf:
    def __init__(self, name, excl=False):
        self.name = name
        self.w = None
        self.rs = {}
        self.dsem = None
        self.excl = excl


class Prog:
    ENGS = ("tensor", "vector", "scalar", "gpsimd", "sync")

    def __init__(self, nc, es):
        self.nc = nc
        self.es = es
        self.nsem = 0
        self.sems = []
        self.esem = {}
        self.waited = {e: {} for e in self.ENGS}
        self.epoch = 0
        self.bg = set()
        self.new_epoch()

    def new_sem(self, name):
        s = Sem(self.es.enter_context(self.nc.semaphore(f"{name}_{self.nsem}")), f"{name}_{self.nsem}")
        self.nsem += 1
        self.sems.append(s)
        return s

    def new_epoch(self):
        for e in self.ENGS:
            self.esem[e] = self.new_sem(e[:2] + str(self.epoch))
        self.epoch += 1

    def _wait(self, eng, toks):
        e = getattr(self.nc, eng)
        wd = self.waited[eng]
        best = {}
        for (s, v) in toks:
            if eng == "tensor" and s is self.esem["tensor"]:
                continue
            if wd.get(s.name, 0) >= v:
                continue
            if best.get(s.name, (None, 0))[1] < v:
                best[s.name] = (s, v)
        for (s, v) in best.values():
            assert v <= s.n
            wd[s.name] = v
            e.wait_ge(s.h, v)

    def op(self, eng, fn, reads=(), writes=(), dma=False):
        toks = []
        for b in reads:
            if b.w is not None:
                toks.append(b.w)
            if b.excl:
                toks.extend(t for t in b.rs.values() if t[0] is not self.esem[eng])
        for b in writes:
            if b.w is not None:
                toks.append(b.w)
            toks.extend(b.rs.values())
        self._wait(eng, toks)
        ins = fn(getattr(self.nc, eng))
        if dma:
            b0 = writes[0] if writes else reads[0]
            if b0.dsem is None:
                b0.dsem = {}
            if eng not in b0.dsem:
                b0.dsem[eng] = self.new_sem("d")
            s = b0.dsem[eng]
            s.n += 16
            ins.then_inc(s.h, 16)
        else:
            s = self.esem[eng]
            s.n += 1
            ins.then_inc(s.h, 1)
        assert s.n < 32000, s.name
        tok = (s, s.n)
        for b in reads:
            b.rs[s.name] = tok
        for b in writes:
            b.w = tok
            b.rs = {}
        return tok

    def barrier(self):
        toks = [(s, s.n) for s in self.sems if s.n > 0 and s.name not in self.bg]
        for e in self.ENGS:
            self._wait(e, toks)


def build(S=8192, L=NL, stop=99):
    NT2 = S // 512
    NT3 = S // 1024
    NKB = S // 128
    HALF2 = NT2 // 2
    HALF3 = NT3 // 2

    nc = bass.Bass("TRN2", target_bir_lowering=False)
    din = lambda n, sh: nc.dram_tensor(n, sh, F32, kind="ExternalInput").ap()
    xT = din("xT", [D, S])
    wqc = din("wqc", [L, D, 2560])
    wkv = din("wkv", [L, D, 384])
    wo = din("wo", [L, D, D])
    wup = din("wup", [L, NJ * 128, 2048])
    wdn = din("wdn", [L, D, DFF])
    vec = din("vec", [128, L, NV])
    cst = din("cst", [128, 544])
    cfg = din("cfg", [128, 8])
    ropeC = din("ropeC", [128, S])
    ropeS = din("ropeS", [128, S])
    yT = nc.dram_tensor("yT", [D, S], F32, kind="ExternalOutput").ap()
    xsA = nc.dram_tensor("xsA", [D, S], F32).ap()
    xsB = nc.dram_tensor("xsB", [D, S], F32).ap()
    wqc_b = nc.dram_tensor("wqc_b", [L, D, 2560], BF16).ap()
    wkv_b = nc.dram_tensor("wkv_b", [L, D, 384], BF16).ap()
    wo_b = nc.dram_tensor("wo_b", [L, D, D], BF16).ap()
    wup_b = nc.dram_tensor("wup_b", [L, NJ * 128, 2048], BF16).ap()
    wdn_b = nc.dram_tensor("wdn_b", [L, D, DFF], BF16).ap()

    fm = lambda ap: ap.rearrange("(c p) t -> p c t", p=128)

    with ExitStack() as es:
        P = Prog(nc, es)
        sb = lambda n, sh, dt: es.enter_context(nc.sbuf_tensor(n, sh, dt))
        ps = es.enter_context(nc.psum_tensor("ps", [128, 4096], F32))
        BK = [Buf(f"bank{i}", excl=True) for i in range(8)]
        bank = lambda i: ps[:, i * 512:(i + 1) * 512]
        class Ring:
            def __init__(self, banks):
                self.banks = list(banks)
                self.i = 0

            def __call__(self):
                b = self.banks[self.i % len(self.banks)]
                self.i += 1
                return b

        nbank = Ring(range(8))

        vec_s = sb("vec_s", [128, L, NV], F32); vec_B = Buf("vec")
        cfg_s = sb("cfg_s", [128, 8], F32); cfg_B = Buf("cfg")
        cstf = sb("cstf", [128, 544], F32); cstf_B = Buf("cstf")
        cstb = sb("cstb", [128, 288], BF16); cstb_B = Buf("cstb")
        KT = sb("KT", [128, S], BF16)
        VS = sb("VS", [128, NKB, 128], BF16)
        KT_B = [Buf(f"KT{i}") for i in range(NT2)]
        VS_B = [Buf(f"VS{i}") for i in range(NT2)]
        bd64 = cstb[:, 0:128]
        onesln = cstb[:, 128:256]
        ones32 = cstb[:, 256:288]
        sel = lambda p: cstf[:, 256 + 128 * p: 256 + 128 * (p + 1)]
        vcol = lambda l, c: vec_s[:, l, c:c + 1]

        sqb = sb("sqb", [128, 512], BF16); sqb_B = Buf("sqb")
        t1 = sb("t1", [128, 512], F32); t1_B = Buf("t1")
        t2 = sb("t2", [128, 512], F32); t2_B = Buf("t2")
        sd = sb("sd", [128, 512], F32); sd_B = Buf("sd")
        rs = sb("rs", [128, 512], F32); rs_B = Buf("rs")
        rC = [sb(f"rC{i}", [128, 512], F32) for i in range(2)]; rC_B = [Buf(f"rC{i}") for i in range(2)]
        rS = [sb(f"rS{i}", [128, 512], F32) for i in range(2)]; rS_B = [Buf(f"rS{i}") for i in range(2)]
        wkv_s = sb("wkv_s", [128, 8, 384], BF16); wkv_B = Buf("wkv_s")
        rope_ctr = [0]
        sdl = sb("sdl", [128, 512], F32); sdl_B = Buf("sdl")
        rsl = sb("rsl", [128, 512], F32); rsl_B = Buf("rsl")
        sd2 = sb("sd2", [128, 512], F32); sd2_B = Buf("sd2")
        rs2 = sb("rs2", [128, 512], F32); rs2_B = Buf("rs2")

        def merge(*gens, head=None):
            gens = list(gens)
            if head is not None:
                gi, n = head
                for _ in range(n):
                    try:
                        next(gens[gi])
                    except StopIteration:
                        gens.pop(gi)
                        break
            while gens:
                for g_ in list(gens):
                    try:
                        next(g_)
                    except StopIteration:
                        gens.remove(g_)

        def run(g_):
            for _ in g_:
                pass

        P.op("sync", lambda e: e.dma_start(out=vec_s[:], in_=vec), writes=[vec_B], dma=True)
        P.op("sync", lambda e: e.dma_start(out=cfg_s[:], in_=cfg), writes=[cfg_B], dma=True)
        P.op("sync", lambda e: e.dma_start(out=cstf[:], in_=cst), writes=[cstf_B], dma=True)
        P.op("vector", lambda e: e.tensor_copy(out=cstb[:, 0:256], in_=cstf[:, 0:256]), reads=[cstf_B], writes=[cstb_B])
        P.op("vector", lambda e: e.tensor_copy(out=cstb[:, 256:288], in_=cstf[:, 512:544]), reads=[cstf_B], writes=[cstb_B])

        P.op("gpsimd", lambda e: e.dma_start(out=wkv_s[:], in_=wkv[0].rearrange("(c p) n -> p c n", p=128)), writes=[wkv_B], dma=True)
        pre2 = ExitStack()
        wqc_s0 = pre2.enter_context(nc.sbuf_tensor("wqc_s_pre", [128, 8, 2560], BF16)); wqc0_B = Buf("wqc_s_pre")
        wo_s0 = pre2.enter_context(nc.sbuf_tensor("wo_s_pre", [128, 8, D], BF16)); wo0_B = Buf("wo_s_pre")
        WB = {}

        def cast_weight(name, src, dst, l, rows):
            b = Buf(f"{name}{l}")
            WB[(name, l)] = b
            for r0 in range(0, rows, 128):
                r1 = min(rows, r0 + 128)
                P.op("gpsimd", lambda e, r0=r0, r1=r1: e.dma_start(out=dst[l, r0:r1, :], in_=src[l, r0:r1, :]),
                     writes=[b], dma=True)
            P.bg.add(b.dsem["gpsimd"].name)

        for l in range(L):
            if l > 0:
                cast_weight("wqc", wqc, wqc_b, l, D)
                cast_weight("wo", wo, wo_b, l, D)
                cast_weight("wkv", wkv, wkv_b, l, D)
            cast_weight("wup", wup, wup_b, l, NJ * 128)
            cast_weight("wdn", wdn, wdn_b, l, D)

        if stop == 0:
            P.barrier()
            return nc
        def mm(out_ap, pairs, reads, writes):
            def fn(e):
                n = len(pairs)
                ins = None
                for i, (lt, r) in enumerate(pairs):
                    ins = e.matmul(out_ap, lhsT=lt, rhs=r, start=(i == 0), stop=(i == n - 1))
                return ins
            return P.op("tensor", fn, reads=reads, writes=writes)

        def load_rope(a):
            slot = rope_ctr[0] % 2
            rope_ctr[0] += 1
            P.op("sync", lambda e: e.dma_start(out=rC[slot][:], in_=ropeC[:, a:a + 512]), writes=[rC_B[slot]], dma=True)
            P.op("sync", lambda e: e.dma_start(out=rS[slot][:], in_=ropeS[:, a:a + 512]), writes=[rS_B[slot]], dma=True)
            return slot

        def rope_norm_gen(l, bq, bqs, gcol, rslot, out_ap, out_B):
            P.op("scalar", lambda e: e.activation(out=sqb[:], in_=bank(bq), func=AF.Square), reads=[BK[bq]], writes=[sqb_B])
            P.op("vector", lambda e: e.scalar_tensor_tensor(out=t1[:], in0=bank(bq), scalar=vcol(l, gcol), in1=rC[rslot][:],
                                                            op0=ALU.mult, op1=ALU.mult),
                 reads=[BK[bq], rC_B[rslot], vec_B], writes=[t1_B])
            yield
            P.op("vector", lambda e: e.scalar_tensor_tensor(out=t2[:], in0=bank(bqs), scalar=vcol(l, gcol + 1), in1=rS[rslot][:],
                                                            op0=ALU.mult, op1=ALU.mult),
                 reads=[BK[bqs], rS_B[rslot], vec_B], writes=[t2_B])
            bm_ = nbank()
            mm(bank(bm_), [(bd64, sqb[:])], reads=[sqb_B, cstb_B], writes=[BK[bm_]])
            yield
            P.op("scalar", lambda e: e.activation(out=sd[:], in_=bank(bm_), func=AF.Sqrt, bias=EPS, scale=1.0),
                 reads=[BK[bm_]], writes=[sd_B])
            P.op("vector", lambda e: e.tensor_tensor(out=t1[:], in0=t1[:], in1=t2[:], op=ALU.add), reads=[t2_B, t1_B], writes=[t1_B])
            yield
            P.op("vector", lambda e: e.reciprocal(out=rs[:], in_=sd[:]), reads=[sd_B], writes=[rs_B])
            P.op("vector", lambda e: e.tensor_tensor(out=out_ap, in0=t1[:], in1=rs[:], op=ALU.mult), reads=[t1_B, rs_B], writes=[out_B])
            yield

        def rope_norm(*a_):
            run(rope_norm_gen(*a_))

        def kv_tile(l, xk, x_B, i):
            a = 512 * i
            rslot = load_rope(a)
            bk_, bks_ = nbank(), nbank()
            mm(bank(bk_), [(wkv_s[:, k, 0:128], xk(k)) for k in range(8)], reads=[wkv_B, x_B], writes=[BK[bk_]])
            mm(bank(bks_), [(wkv_s[:, k, 128:256], xk(k)) for k in range(8)], reads=[wkv_B, x_B], writes=[BK[bks_]])
            rope_norm(l, bk_, bks_, 2, rslot, KT[:, a:a + 512], KT_B[i])
            bv_ = nbank()

            def fnv(e):
                ins = None
                for tb in range(4):
                    for k in range(8):
                        ins = e.matmul(ps[:, bv_ * 512 + tb * 128: bv_ * 512 + (tb + 1) * 128],
                                       lhsT=xk(k)[:, tb * 128:(tb + 1) * 128], rhs=wkv_s[:, k, 256:384],
                                       start=(k == 0), stop=(k == 7))
                return ins
            P.op("tensor", fnv, reads=[wkv_B, x_B], writes=[BK[bv_]])
            P.op("vector", lambda e: e.tensor_copy(out=VS[:, 4 * i:4 * i + 4, :],
                                                   in_=bank(bv_).rearrange("p (a b) -> p a b", b=128)),
                 reads=[BK[bv_]], writes=[VS_B[i]])

        def load_wkv(l):
            P.op("sync", lambda e: e.dma_start(out=wkv_s[:], in_=wkv_b[l].rearrange("(c p) n -> p c n", p=128)),
                 reads=[WB[("wkv", l)]], writes=[wkv_B], dma=True)

        def layernorm_gen(l, X, x_B, gcol0, bcol0, YB, ybf_B, sdt=None, rst=None, ring=None):
            (sd_, sd_B_) = sdt if sdt else (sdl, sdl_B)
            (rs_, rs_B_) = rst if rst else (rsl, rsl_B)
            yl = ybf_B if isinstance(ybf_B, list) else [ybf_B]
            for c in range(8):
                P.op("vector", lambda e, c=c: e.tensor_copy(out=YB(c), in_=X(c)), reads=[x_B[c]], writes=yl)
                if c % 2:
                    yield
            bm_ = (ring or nbank)()
            mm(bank(bm_), [(onesln, YB(c)) for c in range(8)], reads=yl + [cstb_B], writes=[BK[bm_]])
            yield
            for c in range(8):
                P.op("vector", lambda e, c=c: e.tensor_tensor(out=X(c), in0=X(c), in1=bank(bm_), op=ALU.subtract),
                     reads=[BK[bm_], x_B[c]], writes=[x_B[c]])
                P.op("scalar", lambda e, c=c: e.activation(out=YB(c), in_=X(c), func=AF.Square), reads=[x_B[c]], writes=yl)
                if c % 2:
                    yield
            bv_ = (ring or nbank)()
            mm(bank(bv_), [(onesln, YB(c)) for c in range(8)], reads=yl + [cstb_B], writes=[BK[bv_]])
            yield
            P.op("scalar", lambda e: e.activation(out=sd_[:], in_=bank(bv_), func=AF.Sqrt, bias=EPS, scale=1.0),
                 reads=[BK[bv_]], writes=[sd_B_])
            P.op("vector", lambda e: e.reciprocal(out=rs_[:], in_=sd_[:]), reads=[sd_B_], writes=[rs_B_])
            yield
            for c in range(8):
                P.op("vector", lambda e, c=c: e.tensor_tensor(out=X(c), in0=X(c), in1=rs_[:], op=ALU.mult),
                     reads=[rs_B_, x_B[c]], writes=[x_B[c]])
                P.op("scalar", lambda e, c=c: e.activation(out=X(c), in_=X(c), func=AF.Identity, bias=vcol(l, bcol0 + c),
                                                           scale=vcol(l, gcol0 + c)), reads=[x_B[c], vec_B], writes=[x_B[c]])
                if c % 2:
                    yield

        def layernorm(*a_, **k_):
            run(layernorm_gen(*a_, **k_))

        def load_x_halo(dst, dst_B, src, a, T, first, last, bm_lo, bm_hi):
            lo = 1 if first else 0
            hi = T + 1 if last else T + 2
            P.op("sync", lambda e: e.dma_start(out=dst[:, :, lo:hi], in_=fm(src)[:, :, a - 1 + lo:a - 1 + hi]),
                 writes=[dst_B], dma=True)
            if first:
                P.op("vector", lambda e: e.memset(dst[:, :, 0:1], 0.0), writes=[dst_B])
            if last:
                P.op("vector", lambda e: e.memset(dst[:, :, T + 1:T + 2], 0.0), writes=[dst_B])
            if bm_lo:
                P.op("vector", lambda e: e.tensor_scalar(out=dst[:, :, 0:1], in0=dst[:, :, 0:1], scalar1=cfg_s[:, 0:1],
                                                         scalar2=None, op0=ALU.mult), reads=[cfg_B], writes=[dst_B])
            if bm_hi:
                P.op("vector", lambda e: e.tensor_scalar(out=dst[:, :, T + 1:T + 2], in0=dst[:, :, T + 1:T + 2],
                                                         scalar1=cfg_s[:, 0:1], scalar2=None, op0=ALU.mult),
                     reads=[cfg_B], writes=[dst_B])

        with ExitStack() as s0:
            x0 = [s0.enter_context(nc.sbuf_tensor(f"x0f{i}", [128, 8, 512], F32)) for i in range(2)]
            x0_B = [Buf(f"x0f{i}") for i in range(2)]
            x0b = [s0.enter_context(nc.sbuf_tensor(f"x0b{i}", [128, 8, 512], BF16)) for i in range(2)]
            x0b_B = [Buf(f"x0b{i}") for i in range(2)]
            import os
            KSUB = int(os.environ.get('KSUB', '99'))
            stg = [s0.enter_context(nc.sbuf_tensor(f"stg{q}", [128, 2560], F32)) for q in range(2)]
            stg_B = [Buf(f"stg{q}") for q in range(2)]
            wjobs = [(wqc, wqc_s0, wqc0_B, 2560, k) for k in range(8)] + [(wo, wo_s0, wo0_B, D, k) for k in range(8)]

            def stage_weight(n):
                src_, dst_, dB_, ncol, k = wjobs[n]
                q = n % 2
                P.op("sync", lambda e: e.dma_start(out=stg[q][:, 0:ncol], in_=src_[0, k * 128:(k + 1) * 128, :]), writes=[stg_B[q]], dma=True)
                P.op("scalar", lambda e: e.copy(out=dst_[:, k, :], in_=stg[q][:, 0:ncol]), reads=[stg_B[q]], writes=[dB_])
            for i in range(NT2):
                sl = i % 2
                for n in range(len(wjobs) * i // NT2, len(wjobs) * (i + 1) // NT2):
                    stage_weight(n)
                if KSUB < 2: continue
                P.op("sync", lambda e: e.dma_start(out=x0[sl][:], in_=fm(xT)[:, :, 512 * i:512 * i + 512]),
                     writes=[x0_B[sl]], dma=True)
                if KSUB < 3: continue
                P.op("vector", lambda e: e.tensor_copy(out=x0b[sl][:], in_=x0[sl][:]), reads=[x0_B[sl]], writes=[x0b_B[sl]])
                if KSUB < 4: continue
                if KSUB == 4:
                    load_rope(512 * i); continue
                kv_tile(0, lambda k: x0b[sl][:, k, :], x0b_B[sl], i)
            P.barrier()
        if stop == 1:
            return nc

        for l in range(L):
            x_src = xT if l == 0 else xsA
            x_dst = yT if l == L - 1 else xsA
            with ExitStack() as s2:
                a2 = lambda n, sh, dt: s2.enter_context(nc.sbuf_tensor(f"{n}_l{l}", sh, dt))
                if l == 0:
                    wqc_s, wqc_B, wo_s, wo_B = wqc_s0, wqc0_B, wo_s0, wo0_B
                else:
                    wqc_s = a2("wqc_s", [128, 8, 2560], BF16); wqc_B = Buf("wqc_s")
                    wo_s = a2("wo_s", [128, 8, D], BF16); wo_B = Buf("wo_s")
                xf = [a2(f"xf{i}", [128, 8, 514], F32) for i in range(2)]; xf_B = [Buf(f"xf{i}") for i in range(2)]
                xb = a2("xb", [128, 8, 514], BF16); xb_B = Buf("xb")
                QT = a2("QT", [128, 4, 512], BF16); QT_B = [Buf(f"QT{c}") for c in range(4)]
                OT = a2("OT", [128, 4, 512], BF16); OT_B = [Buf(f"OT{c}") for c in range(4)]
                CO = [a2(f"CO{q}", [128, 4, 512], BF16) for q in range(2)]; CO_B = [[Buf(f"CO{q}_{c}") for c in range(4)] for q in range(2)]
                pT = [a2(f"pT{i}", [128, 1024], BF16) for i in range(NP)]; pT_B = [Buf(f"pT{i}") for i in range(NP)]
                ct = a2("ct", [128, 512], F32); ct_B = Buf("ct")
                cth = a2("cth", [128, 2], F32); cth_B = Buf("cth")
                gb = a2("gb", [128, 514], F32); gb_B = Buf("gb")
                acc = a2("acc", [128, 512], F32); acc_B = Buf("acc")
                den_s = a2("den_s", [128, 512], F32); den_B = Buf("den_s")
                rden = a2("rden", [128, 512], F32); rden_B = Buf("rden")

                if l > 0:
                    P.op("sync", lambda e: e.dma_start(out=wqc_s[:], in_=wqc_b[l].rearrange("(c p) n -> p c n", p=128)),
                         reads=[WB[("wqc", l)]], writes=[wqc_B], dma=True)
                    P.op("sync", lambda e: e.dma_start(out=wo_s[:], in_=wo_b[l].rearrange("(c p) n -> p c n", p=128)),
                         reads=[WB[("wo", l)]], writes=[wo_B], dma=True)

                XC_B = [[Buf(f"xc{sl_}_{c}") for c in range(8)] for sl_ in range(2)]

                def prefetch(i):
                    sl_ = i % 2
                    load_x_halo(xf[sl_], xf_B[sl_], x_src, 512 * i, 512, i == 0, i == NT2 - 1, i == HALF2, i == HALF2 - 1)
                    P.op("vector", lambda e: e.tensor_copy(out=xb[:], in_=xf[sl_][:]), reads=[xf_B[sl_]], writes=[xb_B])
                    return load_rope(512 * i)

                def pre_gen(i, rslot):
                    xk = lambda k: xb[:, k, 1:513]
                    COi, COi_B = CO[i % 2], CO_B[i % 2]
                    for c in range(4):
                        bq, bqs = nbank(), nbank()
                        mm(bank(bq), [(wqc_s[:, k, c * 128:(c + 1) * 128], xk(k)) for k in range(8)],
                           reads=[wqc_B, xb_B], writes=[BK[bq]])
                        mm(bank(bqs), [(wqc_s[:, k, 512 + c * 128:512 + (c + 1) * 128], xk(k)) for k in range(8)],
                           reads=[wqc_B, xb_B], writes=[BK[bqs]])
                        yield
                        yield from rope_norm_gen(l, bq, bqs, 0, rslot, QT[:, c, :], QT_B[c])
                    for j in range(4):
                        bB, bC, bh, bH = nbank(), nbank(), nbank(), nbank()
                        cB, cC, ch = 1024 + j * 128, 1536 + j * 128, 2048 + j * 128
                        mm(bank(bC), [(wqc_s[:, k, cC:cC + 128], xk(k)) for k in range(8)], reads=[wqc_B, xb_B], writes=[BK[bC]])
                        mm(bank(bh), [(wqc_s[:, k, ch:ch + 128], xk(k)) for k in range(8)], reads=[wqc_B, xb_B], writes=[BK[bh]])
                        yield

                        def fnh(e, cC=cC, ch=ch, bH=bH):
                            ins = None
                            for (col, off) in ((cC, 0), (ch, 2)):
                                for k in range(8):
                                    ins = e.matmul(ps[:, bH * 512 + off: bH * 512 + off + 2],
                                                   lhsT=wqc_s[:, k, col:col + 128],
                                                   rhs=xb[:, k, 0:514:513], start=(k == 0), stop=(k == 7))
                            return ins
                        P.op("tensor", fnh, reads=[wqc_B, xb_B], writes=[BK[bH]])
                        mm(bank(bB), [(wqc_s[:, k, cB:cB + 128], xk(k)) for k in range(8)], reads=[wqc_B, xb_B], writes=[BK[bB]])
                        P.op("scalar", lambda e: e.copy(out=ct[:], in_=bank(bC)), reads=[BK[bC]], writes=[ct_B])
                        yield
                        P.op("vector", lambda e: e.tensor_tensor(out=gb[:, 1:513], in0=ct[:], in1=bank(bh), op=ALU.mult),
                             reads=[ct_B, BK[bh]], writes=[gb_B])
                        P.op("scalar", lambda e: e.copy(out=cth[:], in_=ps[:, bH * 512:bH * 512 + 2]), reads=[BK[bH]], writes=[cth_B])
                        yield
                        P.op("vector", lambda e: e.tensor_tensor(out=gb[:, 0:1], in0=cth[:, 0:1], in1=ps[:, bH * 512 + 2:bH * 512 + 3],
                                                                 op=ALU.mult), reads=[cth_B, BK[bH]], writes=[gb_B])
                        P.op("vector", lambda e: e.tensor_tensor(out=gb[:, 513:514], in0=cth[:, 1:2], in1=ps[:, bH * 512 + 3:bH * 512 + 4],
                                                                 op=ALU.mult), reads=[cth_B, BK[bH]], writes=[gb_B])
                        w0, w1, w2, bb = 4 + 3 * j, 5 + 3 * j, 6 + 3 * j, 16 + j
                        P.op("vector", lambda e: e.tensor_scalar(out=acc[:], in0=gb[:, 1:513], scalar1=vcol(l, w1), scalar2=vcol(l, bb),
                                                                 op0=ALU.mult, op1=ALU.add), reads=[gb_B, vec_B], writes=[acc_B])
                        yield
                        P.op("vector", lambda e: e.scalar_tensor_tensor(out=acc[:], in0=gb[:, 0:512], scalar=vcol(l, w0), in1=acc[:],
                                                                        op0=ALU.mult, op1=ALU.add), reads=[gb_B, vec_B, acc_B], writes=[acc_B])
                        P.op("vector", lambda e: e.scalar_tensor_tensor(out=acc[:], in0=gb[:, 2:514], scalar=vcol(l, w2), in1=acc[:],
                                                                        op0=ALU.mult, op1=ALU.add), reads=[gb_B, vec_B, acc_B], writes=[acc_B])
                        yield
                        P.op("vector", lambda e: e.tensor_tensor(out=COi[:, j, :], in0=acc[:], in1=bank(bB), op=ALU.mult),
                             reads=[acc_B, BK[bB]], writes=[COi_B[j]])
                        yield

                def post_gen(i):
                    a = 512 * i
                    X, X_B, xc_B = xf[i % 2], xf_B[i % 2], XC_B[i % 2]
                    COi, COi_B = CO[i % 2], CO_B[i % 2]
                    for oc in range(8):
                        bo = nbank()
                        pairs = [(wo_s[:, kc, oc * 128:(oc + 1) * 128], (OT[:, kc, :] if kc < 4 else COi[:, kc - 4, :])) for kc in range(8)]
                        mm(bank(bo), pairs, reads=[wo_B] + OT_B + COi_B, writes=[BK[bo]])
                        P.op("vector", lambda e, oc=oc, bo=bo: e.scalar_tensor_tensor(out=X[:, oc, 1:513], in0=X[:, oc, 1:513], scalar=ALPHA,
                                                                                      in1=bank(bo), op0=ALU.mult, op1=ALU.add),
                             reads=[BK[bo], X_B], writes=[xc_B[oc]])
                        yield
                    yield from layernorm_gen(l, lambda c: X[:, c, 1:513], xc_B, 20, 28,
                                             lambda c: pT[c // 2][:, 512 * (c % 2):512 * (c % 2) + 512], pT_B)
                    P.op("sync", lambda e: e.dma_start(out=fm(xsB)[:, :, a:a + 512], in_=X[:, :, 1:513]), reads=[X_B] + xc_B, dma=True)
                    yield

                rslot_cur = prefetch(0)
                run(pre_gen(0, rslot_cur))
                for i in range(NT2):
                    a = 512 * i
                    if i + 1 < NT2:
                        rslot_next = prefetch(i + 1)
                    qh = 1 if i >= HALF2 else 0
                    pending_norm = [None]
                    for g in range(2):
                        nsteps = 2 * NKB

                        def QK(s):
                            kb, p = divmod(s, 2)
                            c = 2 * g + p
                            b0 = 2 * (s % 2)

                            def fn(e):
                                e.matmul(bank(b0), lhsT=KT[0:64, kb * 128:(kb + 1) * 128], rhs=QT[0:64, c, :], start=True, stop=True)
                                return e.matmul(bank(b0 + 1), lhsT=KT[64:128, kb * 128:(kb + 1) * 128], rhs=QT[64:128, c, :],
                                                start=True, stop=True)
                            P.op("tensor", fn, reads=[KT_B[kb // 4], QT_B[c]], writes=[BK[b0], BK[b0 + 1]])

                        def EXP(s):
                            kb, p = divmod(s, 2)
                            b0 = 2 * (s % 2)
                            kh = 1 if kb >= NKB // 2 else 0
                            bcol = 1 + 2 * qh + kh
                            P.op("scalar", lambda e: e.activation(out=pT[s % NP][:], in_=ps[:, b0 * 512:(b0 + 2) * 512], func=AF.Exp,
                                                                  bias=cfg_s[:, bcol:bcol + 1], scale=0.125),
                                 reads=[BK[b0], BK[b0 + 1], cfg_B], writes=[pT_B[s % NP]])

                        def PV(s):
                            kb, p = divmod(s, 2)
                            first, last = kb == 0, kb == NKB - 1

                            def fn(e):
                                e.matmul(ps[0:64, (4 + p) * 512:(5 + p) * 512], lhsT=VS[:, kb, 0:64], rhs=pT[s % NP][:, 0:512],
                                         start=first, stop=last)
                                ins = e.matmul(ps[64:128, (4 + p) * 512:(5 + p) * 512], lhsT=VS[:, kb, 64:128], rhs=pT[s % NP][:, 512:1024],
                                               start=first, stop=last)
                                if p == 1:
                                    for pp in range(2):
                                        for j in range(2):
                                            ii = 2 * pp + j
                                            ss = 2 * kb + pp
                                            ins = e.matmul(ps[32 * ii:32 * ii + 32, 6 * 512:7 * 512], lhsT=ones32, rhs=pT[ss % NP][:, 512 * j:512 * j + 512],
                                                           start=first, stop=last, tile_position=(0, 32 * ii))
                                return ins
                            rd = [VS_B[kb // 4], pT_B[s % NP], cstb_B]
                            wr = [BK[4 + p]]
                            if p == 1:
                                rd.append(pT_B[(s - 1) % NP])
                                wr.append(BK[6])
                            P.op("tensor", fn, reads=rd, writes=wr)

                        def att_norm(g=g):
                            P.op("vector", lambda e: e.tensor_copy(out=den_s[:], in_=bank(6)), reads=[BK[6]], writes=[den_B])
                            for p in range(2):
                                mm(bank(7), [(sel(p), den_s[:])], reads=[den_B, cstf_B], writes=[BK[7]])
                                P.op("vector", lambda e: e.reciprocal(out=rden[:], in_=bank(7)), reads=[BK[7]], writes=[rden_B])
                                P.op("vector", lambda e, p=p: e.tensor_tensor(out=OT[:, 2 * g + p, :], in0=bank(4 + p), in1=rden[:], op=ALU.mult),
                                     reads=[BK[4 + p], rden_B], writes=[OT_B[2 * g + p]])

                        QK(0); EXP(0); QK(1); EXP(1)
                        if pending_norm[0] is not None:
                            pending_norm[0]()
                            pending_norm[0] = None
                        for s in range(nsteps):
                            if s + 2 < nsteps:
                                QK(s + 2); EXP(s + 2)
                            PV(s)
                        pending_norm[0] = att_norm
                    pending_norm[0]()
                    pending_norm[0] = None
                    if i + 1 < NT2:
                        merge(post_gen(i), pre_gen(i + 1, rslot_next), head=(1, 8))
                    else:
                        run(post_gen(i))
                P.barrier()
            if l == 0:
                pre2.close()
            if stop == 2:
                return nc

            with ExitStack() as s3:
                a3 = lambda n, sh, dt: s3.enter_context(nc.sbuf_tensor(f"{n}_l{l}", sh, dt))
                xm = a3("xm", [128, 8, 1026], F32); xm_B = Buf("xm")
                xmb = a3("xmb", [128, 8, 1026], BF16); xmb_B = Buf("xmb")
                hT = a3("hT", [128, NJ, 1024], BF16); hT_B = [Buf(f"hT{j}") for j in range(NJ)]
                wup_s = [a3(f"wup_s{i}", [128, 8, 256], BF16) for i in range(NW)]; wup_B = [Buf(f"wup_s{i}") for i in range(NW)]
                wdn_s = [a3(f"wdn_s{i}", [128, NJ, 128], BF16) for i in range(2)]; wdn_B = [Buf(f"wdn_s{i}") for i in range(2)]
                ac = [a3(f"ac{i}", [128, 1024], F32) for i in range(4)]; ac_B = [Buf(f"ac{i}") for i in range(4)]
                xmc_B = [[Buf(f"xmc{h_}_{c}") for c in range(8)] for h_ in range(2)]
                ysc_B = [Buf("ysc0"), Buf("ysc1")]
                if l + 1 < L:
                    load_wkv(l + 1)
                for t in range(NT3):
                    a = 1024 * t
                    load_x_halo(xm, xm_B, xsB, a, 1024, t == 0, t == NT3 - 1, t == HALF3, t == HALF3 - 1)
                    P.op("scalar", lambda e: e.copy(out=xmb[:, 0:4, :], in_=xm[:, 0:4, :]), reads=[xm_B], writes=[xmb_B])
                    P.op("vector", lambda e: e.tensor_copy(out=xmb[:, 4:8, :], in_=xm[:, 4:8, :]), reads=[xm_B], writes=[xmb_B])

                    def load_wup(j):
                        ws = j % NW
                        P.op("sync", lambda e: e.dma_start(out=wup_s[ws][:],
                                                           in_=wup_b[l, j * 128:(j + 1) * 128, :].rearrange("p (c n) -> p c n", c=8)),
                             reads=[WB[("wup", l)]], writes=[wup_B[ws]], dma=True)
                    load_wup(0); load_wup(1)
                    for j in range(NJ):
                        if j + 2 < NJ:
                            load_wup(j + 2)
                        ws = j % NW
                        for part in range(2):
                            slot = (2 * j + part) % 3
                            b0 = 2 * slot
                            hbk = 6 + (2 * j + part) % 2
                            hbB = BK[hbk]
                            hoff = (6 + (2 * j + part) % 2) * 512
                            A = ac[(2 * j + part) % 4]
                            A_B = ac_B[(2 * j + part) % 4]

                            def fnu(e, part=part, b0=b0, ws=ws):
                                ins = None
                                for hf in range(2):
                                    for k in range(8):
                                        ins = e.matmul(bank(b0 + hf), lhsT=wup_s[ws][:, k, part * 128:(part + 1) * 128],
                                                       rhs=xmb[:, k, 1 + 512 * hf:513 + 512 * hf], start=(k == 0), stop=(k == 7))
                                return ins

                            def fnuh(e, part=part, hoff=hoff, ws=ws):
                                ins = None
                                for k in range(8):
                                    ins = e.matmul(ps[:, hoff:hoff + 2], lhsT=wup_s[ws][:, k, part * 128:(part + 1) * 128],
                                                   rhs=xmb[:, k, 0:1026:1025], start=(k == 0), stop=(k == 7))
                                return ins
                            P.op("tensor", fnu, reads=[wup_B[ws], xmb_B], writes=[BK[b0], BK[b0 + 1]])
                            P.op("tensor", fnuh, reads=[wup_B[ws], xmb_B], writes=[hbB])
                            vc = 52 + (2 * j + part) * 4
                            U = ps[:, b0 * 512:(b0 + 2) * 512]
                            P.op("scalar", lambda e, U=U, A=A, vc=vc: e.activation(out=A[:], in_=U, func=AF.Identity, bias=vcol(l, vc + 3),
                                                                                  scale=vcol(l, vc + 1)),
                                 reads=[BK[b0], BK[b0 + 1], vec_B], writes=[A_B])
                            P.op("vector", lambda e, A=A, vc=vc, hoff=hoff: e.scalar_tensor_tensor(out=A[:, 0:1], in0=ps[:, hoff:hoff + 1], scalar=vcol(l, vc),
                                                                                                  in1=A[:, 0:1], op0=ALU.mult, op1=ALU.add),
                                 reads=[hbB, vec_B, A_B], writes=[A_B])
                            P.op("vector", lambda e, A=A, vc=vc, hoff=hoff: e.scalar_tensor_tensor(out=A[:, 1023:1024], in0=ps[:, hoff + 1:hoff + 2],
                                                                                                  scalar=vcol(l, vc + 2), in1=A[:, 1023:1024],
                                                                                                  op0=ALU.mult, op1=ALU.add),
                                 reads=[hbB, vec_B, A_B], writes=[A_B])
                            P.op("vector", lambda e, U=U, A=A, vc=vc: e.scalar_tensor_tensor(out=A[:, 1:1024], in0=U[:, 0:1023], scalar=vcol(l, vc),
                                                                                            in1=A[:, 1:1024], op0=ALU.mult, op1=ALU.add),
                                 reads=[BK[b0], BK[b0 + 1], vec_B, A_B], writes=[A_B])
                            P.op("vector", lambda e, U=U, A=A, vc=vc: e.scalar_tensor_tensor(out=A[:, 0:1023], in0=U[:, 1:1024], scalar=vcol(l, vc + 2),
                                                                                            in1=A[:, 0:1023], op0=ALU.mult, op1=ALU.add),
                                 reads=[BK[b0], BK[b0 + 1], vec_B, A_B], writes=[A_B])
                        Ag, Ag_B = ac[(2 * j) % 4], ac_B[(2 * j) % 4]
                        Av, Av_B = ac[(2 * j + 1) % 4], ac_B[(2 * j + 1) % 4]
                        P.op("scalar", lambda e, Ag=Ag: e.activation(out=Ag[:], in_=Ag[:], func=AF.Silu), reads=[Ag_B], writes=[Ag_B])
                        P.op("gpsimd", lambda e, Ag=Ag, Av=Av, j=j: e.tensor_tensor(out=hT[:, j, :], in0=Ag[:], in1=Av[:], op=ALU.mult),
                             reads=[Ag_B, Av_B], writes=[hT_B[j]])

                    ring_dn = Ring([0, 1, 2, 3, 4, 5])
                    ring_ln = Ring([6, 7])

                    def load_wdn(n):
                        oc = n % 8
                        P.op("sync", lambda e: e.dma_start(out=wdn_s[n % 2][:],
                                                           in_=wdn_b[l, oc * 128:(oc + 1) * 128, :].rearrange("p (j n) -> p j n", j=NJ)),
                             reads=[WB[("wdn", l)]], writes=[wdn_B[n % 2]], dma=True)

                    def down_gen(hf):
                        for oc in range(8):
                            n = 8 * hf + oc
                            if n + 1 < 16:
                                load_wdn(n + 1)
                            bo = ring_dn()
                            mm(bank(bo), [(wdn_s[n % 2][:, j, :], hT[:, j, 512 * hf:512 * hf + 512]) for j in range(NJ)],
                               reads=[wdn_B[n % 2]] + hT_B, writes=[BK[bo]])
                            P.op("vector", lambda e, oc=oc, bo=bo, hf=hf: e.scalar_tensor_tensor(
                                out=xm[:, oc, 1 + 512 * hf:513 + 512 * hf], in0=xm[:, oc, 1 + 512 * hf:513 + 512 * hf], scalar=ALPHA,
                                in1=bank(bo), op0=ALU.mult, op1=ALU.add), reads=[BK[bo], xm_B], writes=[xmc_B[hf][oc]])
                            yield
                    load_wdn(0)
                    run(down_gen(0))
                    merge(down_gen(1),
                          layernorm_gen(l, lambda c: xm[:, c, 1:513], xmc_B[0], 36, 44, lambda c: xmb[:, c, 1:513], ysc_B[0], ring=ring_ln))
                    run(layernorm_gen(l, lambda c: xm[:, c, 513:1025], xmc_B[1], 36, 44, lambda c: xmb[:, c, 513:1025], ysc_B[1],
                                      sdt=(sd2, sd2_B), rst=(rs2, rs2_B), ring=ring_ln))
                    allc = xmc_B[0] + xmc_B[1]
                    P.op("sync", lambda e: e.dma_start(out=fm(x_dst)[:, :, a:a + 1024], in_=xm[:, :, 1:1025]), reads=[xm_B] + allc, dma=True)
                    if l + 1 < L:
                        P.op("scalar", lambda e: e.copy(out=xmb[:, 0:4, 1:1025], in_=xm[:, 0:4, 1:1025]), reads=[xm_B] + allc, writes=[xmb_B] + ysc_B)
                        P.op("vector", lambda e: e.tensor_copy(out=xmb[:, 4:8, 1:1025], in_=xm[:, 4:8, 1:1025]), reads=[xm_B] + allc, writes=[xmb_B] + ysc_B)
                        for hf in range(2):
                            kv_tile(l + 1, lambda k, hf=hf: xmb[:, k, 1 + 512 * hf:513 + 512 * hf], xmb_B, 2 * t + hf)
                P.barrier()
            if l + 1 < L and P.esem["tensor"].n > 8000:
                P.new_epoch()
    return nc


def _layout_weights(w_in, q_norm, k_norm, conv_w, conv_b, w_o, ln1_g, ln1_b, w_up, ffn_conv_w, ffn_conv_b, w_down,
                    ln2_g, ln2_b, L):
    f = np.float32
    qperm = np.array([(c if p < 64 else 4 + c) * 64 + (p % 64) for c in range(4) for p in range(128)])
    qperm_sw = np.array([(c if p < 64 else 4 + c) * 64 + ((p % 64) ^ 1) for c in range(4) for p in range(128)])
    kperm_sw = np.array([512 + (p ^ 1) for p in range(128)])
    wqc = np.ascontiguousarray(np.concatenate([w_in[:, :, qperm], w_in[:, :, qperm_sw], w_in[:, :, 768:2304]], axis=2), dtype=f)
    wkv = np.ascontiguousarray(np.concatenate([w_in[:, :, 512:640], w_in[:, :, kperm_sw], w_in[:, :, 640:768]], axis=2), dtype=f)
    wo = np.ascontiguousarray(np.concatenate([w_o[:, qperm, :], w_o[:, 512:, :]], axis=1), dtype=f)
    gv = np.concatenate([w_up[:, :, :DFF].reshape(L, 8, 128, NJ, 128), w_up[:, :, DFF:].reshape(L, 8, 128, NJ, 128)], axis=4)
    wup = np.ascontiguousarray(gv.transpose(0, 3, 2, 1, 4).reshape(L, NJ * 128, 2048), dtype=f)
    wdn = np.ascontiguousarray(w_down.reshape(L, NJ, 128, 8, 128).transpose(0, 3, 2, 1, 4).reshape(L, D, DFF), dtype=f)
    vec = np.zeros((128, L, NV), f)
    p = np.arange(128)
    for l in range(L):
        vec[:, l, 0] = q_norm[l][p % 64]
        vec[:, l, 1] = q_norm[l][(p % 64) ^ 1]
        vec[:, l, 2] = k_norm[l][p % 64]
        vec[:, l, 3] = k_norm[l][(p % 64) ^ 1]
        for j in range(4):
            for tap in range(3):
                vec[:, l, 4 + 3 * j + tap] = conv_w[l, tap, j * 128 + p]
            vec[:, l, 16 + j] = conv_b[l, j * 128 + p]
        for c in range(8):
            vec[:, l, 20 + c] = ln1_g[l, c * 128 + p]
            vec[:, l, 28 + c] = ln1_b[l, c * 128 + p]
            vec[:, l, 36 + c] = ln2_g[l, c * 128 + p]
            vec[:, l, 44 + c] = ln2_b[l, c * 128 + p]
        for j in range(NJ):
            for part in range(2):
                base = 52 + (2 * j + part) * 4
                for tap in range(3):
                    vec[:, l, base + tap] = ffn_conv_w[l, tap, part * DFF + j * 128 + p]
                vec[:, l, base + 3] = ffn_conv_b[l, part * DFF + j * 128 + p]
    return wqc, wkv, wo, wup, wdn, vec


def _consts():
    f = np.float32
    cst = np.zeros((128, 544), f)
    p = np.arange(128)
    cst[:, 0:128] = (p[:, None] // 64 == p[None, :] // 64) / 64.0
    cst[:, 128:256] = 1.0 / 1024.0
    for pr in range(2):
        for m in range(128):
            cst[32 * (2 * pr + m // 64), 256 + 128 * pr + m] = 1.0
    cst[:, 512:544] = 1.0
    return cst


def _rope_tables(seq_len, S):
    f = np.float32
    t = np.arange(S) % seq_len
    row = (t // 64).astype(f)
    col = (t % 64).astype(f)
    inv = (f(10000.0) ** (-np.arange(16, dtype=f) / f(16))).astype(f)
    ang = np.concatenate([row[:, None] * inv, col[:, None] * inv], axis=-1).astype(f)
    cos, sin = np.cos(ang).astype(f), np.sin(ang).astype(f)
    d = np.arange(128) % 64
    C = cos[:, d // 2].T
    Sg = sin[:, d // 2].T * np.where(d % 2 == 0, -1.0, 1.0)[:, None]
    return np.ascontiguousarray(C, dtype=f), np.ascontiguousarray(Sg, dtype=f)


def _core_inputs(x_prompt, x_sample, S):
    maps = []
    nb_s = x_sample.shape[0]
    for c in range(nb_s):
        maps.append((np.ascontiguousarray(x_sample[c].T), S))
    for c in range(x_prompt.shape[0] // 2):
        xx = np.concatenate([x_prompt[2 * c], x_prompt[2 * c + 1]], axis=0)
        maps.append((np.ascontiguousarray(xx.T), S // 2))
    return maps


_NC_CACHE = {}


def _run(x_prompt, x_sample, weights, S, L):
    wqc, wkv, wo, wup, wdn, vec = _layout_weights(*weights, L)
    cst = _consts()
    cores = _core_inputs(x_prompt, x_sample, S)
    in_maps = []
    tabs = {}
    for (xT, seqlen) in cores:
        if seqlen not in tabs:
            tabs[seqlen] = _rope_tables(seqlen, S)
        C, Sg = tabs[seqlen]
        cfg = np.zeros((128, 8), np.float32)
        two = seqlen < S
        cfg[:, 0] = 0.0 if two else 1.0
        cfg[:, 2] = NEG if two else 0.0
        cfg[:, 3] = NEG if two else 0.0
        in_maps.append(dict(xT=xT, wqc=wqc, wkv=wkv, wo=wo, wup=wup, wdn=wdn, vec=vec, cst=cst, cfg=cfg, ropeC=C, ropeS=Sg))
    key = (S, L)
    if key not in _NC_CACHE:
        import os
        _NC_CACHE[key] = build(S, L, int(os.environ.get('KSTOP', '99')))
    nc = _NC_CACHE[key]
    res = run_bass_kernel_spmd(nc, in_maps, core_ids=list(range(len(in_maps))))
    return [np.asarray(r["yT"]) for r in res.results]


def kernel(x_prompt, x_sample, w_in, q_norm, k_norm, conv_w, conv_b, w_o, ln1_g, ln1_b,
           w_up, ffn_conv_w, ffn_conv_b, w_down, ln2_g, ln2_b):
    a = lambda v: np.asarray(v, dtype=np.float32)
    x_prompt, x_sample = a(x_prompt), a(x_sample)
    weights = [a(v) for v in (w_in, q_norm, k_norm, conv_w, conv_b, w_o, ln1_g, ln1_b, w_up, ffn_conv_w, ffn_conv_b,
                              w_down, ln2_g, ln2_b)]
    S = x_sample.shape[1]
    outs = _run(x_prompt, x_sample, weights, S, NL)
    nb_s = x_sample.shape[0]
    y_sample = np.stack([outs[c].T for c in range(nb_s)], axis=0)
    yp = []
    for c in range(x_prompt.shape[0] // 2):
        o = outs[nb_s + c].T
        yp.append(o[:S // 2])
        yp.append(o[S // 2:])
    y_prompt = np.stack(yp, axis=0)
    return (np.ascontiguousarray(y_prompt, dtype=np.float32), np.ascontiguousarray(y_sample, dtype=np.float32))
```
